# Optimizing a Trainium2 kernel written in Bass

```python
import jax, jax.numpy as jnp
from jax import lax
import numpy as np

D_MODEL = 1024
BATCH = 8
SEQ = 8192
DEPTH = 2

N_EVEN = (DEPTH + 1) // 2
N_ODD = DEPTH // 2
BLOCK = 128
D_FF = 2816
MIX_WIDTH = D_MODEL
EPS = 1e-6
A_WIDTH = MIX_WIDTH // 2
A_GROUPS = 4
A_GROUP_DIM = A_WIDTH // A_GROUPS
B_WIDTH = MIX_WIDTH // 2
B_HEADS = 4
B_HEAD_DIM = B_WIDTH // B_HEADS
ROPE_BASE = 10000.0
AB_IN = 2 * A_WIDTH + 4 * B_WIDTH
C_HEADS = 16
C_HEAD_DIM = MIX_WIDTH // C_HEADS

kernel_name = "hybrid_gmlp_retention_stickbreaking_macaron"


def rms_norm(x, g):
    xf = x.astype(jnp.float32)
    y = xf * lax.rsqrt(jnp.mean(xf * xf, axis=-1, keepdims=True) + EPS)
    return (y * g.astype(jnp.float32)).astype(x.dtype)


def layer_norm(x, g, b):
    xf = x.astype(jnp.float32)
    mu = jnp.mean(xf, axis=-1, keepdims=True)
    var = jnp.mean(jnp.square(xf - mu), axis=-1, keepdims=True)
    y = (xf - mu) * lax.rsqrt(var + EPS)
    return (y * g.astype(jnp.float32) + b.astype(jnp.float32)).astype(x.dtype)


def swiglu(x, w_gate, w_up, w_down):
    return (jax.nn.silu(x @ w_gate) * (x @ w_up)) @ w_down


def rotary(x, positions):
    half = x.shape[-1] // 2
    inv = ROPE_BASE ** (-jnp.arange(half, dtype=jnp.float32) / half)
    ang = positions.astype(jnp.float32)[:, None] * inv[None, :]
    cos = jnp.cos(ang)[None, :, None, :]
    sin = jnp.sin(ang)[None, :, None, :]
    xf = x.astype(jnp.float32)
    x1, x2 = xf[..., :half], xf[..., half:]
    return jnp.concatenate([x1 * cos - x2 * sin, x1 * sin + x2 * cos], axis=-1).astype(x.dtype)


def gmlp_mixer(a, v_g, v_b, w_s, b_s):
    bsz, s_len, _ = a.shape
    u, v = a[..., :A_WIDTH], a[..., A_WIDTH:]
    v = layer_norm(v, v_g, v_b)
    v = v.reshape(bsz, s_len // BLOCK, BLOCK, A_GROUPS, A_GROUP_DIM)
    causal = jnp.tril(jnp.ones((BLOCK, BLOCK), dtype=bool))
    w = jnp.where(causal[None], w_s, jnp.zeros_like(w_s))
    s = jnp.einsum('gts,bcsgd->bctgd', w, v) + b_s.T[None, None, :, :, None]
    return u * s.reshape(bsz, s_len, A_WIDTH)


def retention_mixer(q, k, v, g, norm_g):
    bsz, s_len, _ = q.shape
    n_chunks = s_len // BLOCK
    pos = jnp.arange(s_len)
    q = rotary(q.reshape(bsz, s_len, B_HEADS, B_HEAD_DIM), pos)
    k = rotary(k.reshape(bsz, s_len, B_HEADS, B_HEAD_DIM), pos) * (B_HEAD_DIM ** -0.5)
    v = v.reshape(bsz, s_len, B_HEADS, B_HEAD_DIM)

    def chunks(t):
        return t.reshape(bsz, n_chunks, BLOCK, B_HEADS, B_HEAD_DIM).transpose(1, 0, 3, 2, 4).astype(jnp.float32)

    log_gamma = jnp.log1p(-(2.0 ** (-5.0 - jnp.arange(B_HEADS, dtype=jnp.float32))))
    idx = jnp.arange(BLOCK, dtype=jnp.float32)
    diff = idx[:, None] - idx[None, :]
    decay = jnp.where(diff >= 0, jnp.exp(log_gamma[:, None, None] * jnp.maximum(diff, 0.0)), 0.0)
    xi = jnp.exp(log_gamma[:, None] * (idx + 1.0))[:, :, None]
    zeta = jnp.exp(log_gamma[:, None] * (BLOCK - 1.0 - idx))[:, :, None]
    gamma_c = jnp.exp(log_gamma * BLOCK)[:, None, None]

    def step(state, qkv):
        qc, kc, vc = qkv
        scores = jnp.einsum('bhnd,bhmd->bhnm', qc, kc) * decay
        inner = jnp.einsum('bhnm,bhme->bhne', scores, vc)
        cross = jnp.einsum('bhnd,bhde->bhne', qc * xi, state)
        state = gamma_c * state + jnp.einsum('bhmd,bhme->bhde', kc * zeta, vc)
        return state, inner + cross

    state0 = jnp.zeros((bsz, B_HEADS, B_HEAD_DIM, B_HEAD_DIM), jnp.float32)
    _, out = lax.scan(step, state0, (chunks(q), chunks(k), chunks(v)))
    out = out.transpose(1, 0, 3, 2, 4).reshape(bsz, s_len, B_HEADS, B_HEAD_DIM)
    out = rms_norm(out, norm_g.reshape(B_HEADS, B_HEAD_DIM))
    y = jax.nn.silu(g.astype(jnp.float32)) * out.reshape(bsz, s_len, B_WIDTH)
    return y.astype(g.dtype)


def stick_breaking_mixer(q, k, v):
    in_dtype = q.dtype
    bsz, s_len, _ = q.shape
    n_blocks = s_len // BLOCK

    def heads(t):
        return t.reshape(bsz, s_len, C_HEADS, C_HEAD_DIM).transpose(0, 2, 1, 3).astype(jnp.float32)

    q = heads(q) * (C_HEAD_DIM ** -0.5)
    k = heads(k)
    v = heads(v)
    idx = jnp.arange(BLOCK)
    outs = []
    for i in range(n_blocks):
        qi = q[:, :, i * BLOCK:(i + 1) * BLOCK]
        kb = k[:, :, :(i + 1) * BLOCK].reshape(bsz, C_HEADS, i + 1, BLOCK, C_HEAD_DIM).transpose(2, 0, 1, 3, 4)
        vb = v[:, :, :(i + 1) * BLOCK].reshape(bsz, C_HEADS, i + 1, BLOCK, C_HEAD_DIM).transpose(2, 0, 1, 3, 4)
        starts = jnp.arange(i + 1) * BLOCK
        t_pos = i * BLOCK + idx

        def step(carry, inp, qi=qi, t_pos=t_pos):
            acc, o = carry
            kj, vj, start = inp
            z = jnp.einsum('bhtd,bhsd->bhts', qi, kj)
            valid = (start + idx)[None, :] < t_pos[:, None]
            log_1m = jnp.where(valid, jax.nn.log_sigmoid(-z), 0.0)
            rev = lax.cumsum(log_1m, axis=3, reverse=True)
            log_a = jax.nn.log_sigmoid(z) + (rev - log_1m) + acc[..., None]
            a = jnp.where(valid, jnp.exp(log_a), 0.0)
            o = o + jnp.einsum('bhts,bhsd->bhtd', a, vj)
            return (acc + rev[..., 0], o), None

        init = (jnp.zeros((bsz, C_HEADS, BLOCK), jnp.float32),
                jnp.zeros((bsz, C_HEADS, BLOCK, C_HEAD_DIM), jnp.float32))
        (_, o), _ = lax.scan(step, init, (kb, vb, starts), reverse=True)
        outs.append(o)
    out = jnp.concatenate(outs, axis=2)
    return out.transpose(0, 2, 1, 3).reshape(bsz, s_len, MIX_WIDTH).astype(in_dtype)


def even_mixer(h, w_in, w_out, v_g, v_b, w_s, b_s, ret_g):
    p = h @ w_in
    a = jax.nn.gelu(p[..., :2 * A_WIDTH])
    q_r, k_r, v_r, g_r = jnp.split(p[..., 2 * A_WIDTH:], 4, axis=-1)
    y = jnp.concatenate([gmlp_mixer(a, v_g, v_b, w_s, b_s),
                         retention_mixer(q_r, k_r, v_r, g_r, ret_g)], axis=-1)
    return y @ w_out


def odd_mixer(h, w_qkv, w_out):
    q, k, v = jnp.split(h @ w_qkv, 3, axis=-1)
    return stick_breaking_mixer(q, k, v) @ w_out


def setup_inputs(seed: int = 0) -> dict:
    key = jax.random.key(seed)
    ks = jax.random.split(key, 16)
    f32 = jnp.float32
    nrm = lambda k, shape: jax.random.normal(k, shape, f32)
    return {
        "x": nrm(ks[0], (BATCH, SEQ, D_MODEL)),
        "norm_g": 1.0 + 0.05 * nrm(ks[1], (DEPTH, 6, D_MODEL)),
        "ffn_w_gate": nrm(ks[2], (DEPTH, 2, D_MODEL, D_FF)) * D_MODEL ** -0.5,
        "ffn_w_up": nrm(ks[3], (DEPTH, 2, D_MODEL, D_FF)) * D_MODEL ** -0.5,
        "ffn_w_down": nrm(ks[4], (DEPTH, 2, D_FF, D_MODEL)) * D_FF ** -0.5,
        "ab_w_in": nrm(ks[5], (N_EVEN, D_MODEL, AB_IN)) * D_MODEL ** -0.5,
        "ab_w_out": nrm(ks[6], (N_EVEN, MIX_WIDTH, D_MODEL)) * MIX_WIDTH ** -0.5,
        "gmlp_v_norm_g": 1.0 + 0.05 * nrm(ks[7], (N_EVEN, A_WIDTH)),
        "gmlp_v_norm_b": 0.02 * nrm(ks[8], (N_EVEN, A_WIDTH)),
        "gmlp_w_s": nrm(ks[9], (N_EVEN, A_GROUPS, BLOCK, BLOCK)) * BLOCK ** -0.5,
        "gmlp_b_s": 1.0 + 0.02 * nrm(ks[10], (N_EVEN, A_GROUPS, BLOCK)),
        "ret_norm_g": 1.0 + 0.05 * nrm(ks[11], (N_EVEN, B_WIDTH)),
        "sb_w_qkv": nrm(ks[12], (N_ODD, D_MODEL, 3 * MIX_WIDTH)) * D_MODEL ** -0.5,
        "sb_w_out": nrm(ks[13], (N_ODD, MIX_WIDTH, D_MODEL)) * MIX_WIDTH ** -0.5,
    }


def reference(x, norm_g, ffn_w_gate, ffn_w_up, ffn_w_down, ab_w_in, ab_w_out,
              gmlp_v_norm_g, gmlp_v_norm_b, gmlp_w_s, gmlp_b_s, ret_norm_g,
              sb_w_qkv, sb_w_out):
    for layer in range(DEPTH):
        g = norm_g[layer]
        f = swiglu(rms_norm(x, g[0]), ffn_w_gate[layer, 0], ffn_w_up[layer, 0], ffn_w_down[layer, 0])
        x = x + 0.5 * rms_norm(f, g[1])
        h = rms_norm(x, g[2])
        if layer % 2 == 0:
            e = layer // 2
            m = even_mixer(h, ab_w_in[e], ab_w_out[e], gmlp_v_norm_g[e], gmlp_v_norm_b[e],
                           gmlp_w_s[e], gmlp_b_s[e], ret_norm_g[e])
        else:
            o = layer // 2
            m = odd_mixer(h, sb_w_qkv[o], sb_w_out[o])
        x = x + rms_norm(m, g[3])
        f = swiglu(rms_norm(x, g[4]), ffn_w_gate[layer, 1], ffn_w_up[layer, 1], ffn_w_down[layer, 1])
        x = x + 0.5 * rms_norm(f, g[5])
    return x
```

```python
import contextlib
import numpy as np
import concourse.bass as bass
import concourse.mybir as mybir
from concourse.bass_utils import run_bass_kernel_spmd

F32 = mybir.dt.float32
BF16 = mybir.dt.bfloat16
AF = mybir.ActivationFunctionType
ALU = mybir.AluOpType
AX = mybir.AxisListType

D = 1024
KC = 8
FF = 2816
FC = 22
TT = 512
EPS = 1e-6
NEG = -30000.0
FAST_RECIP = False
USE_LNEXP = True
USE_DIV = False
USE_ARS = False


def _pack(W, nw):
    K, M = W.shape
    return np.ascontiguousarray(W.reshape(K // 128, 128, M // nw, nw).transpose(1, 2, 0, 3)).reshape(128, -1)


class WLayout:
    def __init__(self):
        self.off = {}
        self.n = 0

    def add(self, name, ncols):
        self.off[name] = (self.n, ncols)
        self.n += ncols


def weight_layout():
    L = WLayout()
    for l in range(2):
        for f in range(2):
            L.add(("gu", l, f), FC * 2 * KC * 128)
            L.add(("dn", l, f), KC * FC * 128)
        if l == 0:
            L.add(("in_u", 0), 4 * KC * 128)
            L.add(("in_g", 0), 4 * KC * 128)
            L.add(("in_tok", 0), 8 * KC * 256)
            L.add(("out", 0), KC * KC * 128)
        else:
            L.add(("q", 1), KC * KC * 128)
            L.add(("k", 1), KC * KC * 128)
            L.add(("v", 1), 4 * KC * 256)
            L.add(("out", 1), KC * KC * 128)
    return L


def pack_weights(inp):
    L = weight_layout()
    wall = np.empty((128, L.n), np.float32)

    def put(name, arr):
        o, n = L.off[name]
        assert arr.shape == (128, n), (name, arr.shape, n)
        wall[:, o:o + n] = arr

    for l in range(2):
        for f in range(2):
            g = _pack(inp["ffn_w_gate"][l, f], 128).reshape(128, FC, 1, KC * 128)
            u = _pack(inp["ffn_w_up"][l, f], 128).reshape(128, FC, 1, KC * 128)
            put(("gu", l, f), np.concatenate([g, u], axis=2).reshape(128, -1))
            put(("dn", l, f), _pack(inp["ffn_w_down"][l, f], 128))
    win = inp["ab_w_in"][0]
    put(("in_u", 0), _pack(win[:, 0:512], 128))
    put(("in_g", 0), _pack(win[:, 2560:3072], 128))
    put(("in_tok", 0), _pack(win[:, 512:2560], 256))
    put(("out", 0), _pack(inp["ab_w_out"][0], 128))
    wq = inp["sb_w_qkv"][0]
    put(("q", 1), _pack(wq[:, 0:1024], 128))
    put(("k", 1), _pack(wq[:, 1024:2048], 128))
    put(("v", 1), _pack(wq[:, 2048:3072], 256))
    put(("out", 1), _pack(inp["sb_w_out"][0], 128))
    return wall


CONST_COLS = {}
_c = 0
for _n, _w in [("G", 96), ("RETG", 4), ("ZETA", 4), ("GAMC", 4), ("LNG", 512), ("LNB", 512), ("BS", 512),
               ("DECT", 512), ("XIBC", 512), ("WST", 512), ("M01", 128), ("MB", 128), ("NEGU", 128),
               ("ID", 128)]:
    CONST_COLS[_n] = (_c, _w)
    _c += _w
NCONST = _c


def pack_consts(inp):
    c = np.zeros((128, NCONST), np.float32)

    def put(name, arr):
        o, n = CONST_COLS[name]
        c[:, o:o + n] = arr

    g = inp["norm_g"].reshape(12, KC, 128)
    put("G", g.transpose(2, 0, 1).reshape(128, 96))
    put("RETG", inp["ret_norm_g"][0].reshape(4, 128).T)
    lg = np.log1p(-(2.0 ** (-5.0 - np.arange(4, dtype=np.float64))))
    idx = np.arange(128, dtype=np.float64)
    scale = 128.0 ** -0.5
    zeta = np.exp(lg[:, None] * (127.0 - idx)) * scale
    put("ZETA", zeta.T.astype(np.float32))
    put("GAMC", np.broadcast_to(np.exp(lg * 128.0)[None, :], (128, 4)).astype(np.float32))
    put("LNG", np.broadcast_to(inp["gmlp_v_norm_g"][0][None, :], (128, 512)))
    put("LNB", np.broadcast_to(inp["gmlp_v_norm_b"][0][None, :], (128, 512)))
    put("BS", np.broadcast_to(inp["gmlp_b_s"][0].reshape(1, 512), (128, 512)))
    diff = idx[:, None] - idx[None, :]
    dec = np.where(diff >= 0, np.exp(lg[:, None, None] * np.maximum(diff, 0.0)), 0.0) * scale
    put("DECT", dec.transpose(2, 0, 1).reshape(128, 512).astype(np.float32))
    xi = np.exp(lg[:, None] * (idx + 1.0))
    put("XIBC", np.broadcast_to(xi.reshape(1, 512), (128, 512)).astype(np.float32))
    put("WST", inp["gmlp_w_s"][0].transpose(2, 0, 1).reshape(128, 512))
    s_i = np.arange(128)[:, None]
    t_i = np.arange(128)[None, :]
    put("M01", (t_i >= s_i).astype(np.float32))
    put("MB", np.where(s_i >= t_i, NEG, 0.0).astype(np.float32))
    put("NEGU", np.where(s_i >= t_i, -1.0, 0.0).astype(np.float32))
    put("ID", np.eye(128, dtype=np.float32))
    return c


def rope_tables(S):
    half = 64
    inv = (10000.0 ** (-np.arange(half, dtype=np.float32) / half)).astype(np.float32)
    ang = np.arange(S, dtype=np.float32)[:, None] * inv[None, :]
    cos = np.cos(ang).astype(np.float32).reshape(S // 128, 128, half).transpose(1, 0, 2)
    sin = np.sin(ang).astype(np.float32).reshape(S // 128, 128, half).transpose(1, 0, 2)
    return np.ascontiguousarray(cos), np.ascontiguousarray(sin)


class Buf:
    __slots__ = ("name", "lw", "rd", "excl")

    def __init__(self, name="", excl=False):
        self.name = name
        self.lw = None
        self.rd = []
        self.excl = excl


class Op:
    __slots__ = ("eng", "fn", "deps", "dma", "key", "nd", "signal", "tok", "idx")


class Sched:
    ENGS = ("sync", "scalar", "vector", "tensor", "gpsimd")

    def __init__(self):
        self.ops = {e: [] for e in self.ENGS}
        self.n = 0
        self.finals = []

    def add(self, eng, fn, r=(), w=(), key=None, nd=1):
        op = Op()
        op.eng, op.fn, op.dma, op.key, op.nd, op.signal, op.tok = eng, fn, key is not None, key, nd, False, None
        op.idx = self.n
        self.n += 1
        if any(b.excl for b in r):
            w = list(w) + [b for b in r if b.excl]
            r = [b for b in r if not b.excl]
        deps = {}
        for b in r:
            if b.lw is not None:
                deps[b.lw.idx] = (b.lw, True)
        for b in w:
            if b.lw is not None and b.lw.idx not in deps:
                deps[b.lw.idx] = (b.lw, False)
            for o in b.rd:
                if o.idx not in deps:
                    deps[o.idx] = (o, False)
        final = []
        for d, raw in deps.values():
            if d is op:
                continue
            if (not d.dma) and (not op.dma) and d.eng == eng and not raw and eng == "tensor":
                continue
            final.append(d)
            d.signal = True
        op.deps = final
        for b in r:
            b.rd.append(op)
        for b in w:
            b.lw = op
            b.rd = []
        self.ops[eng].append(op)
        return op

    def emit(self, nc, block, es):
        engsem = {}
        for e in self.ENGS:
            engsem[e] = es.enter_context(nc.semaphore("e_" + e))
        keysem = {}
        keycnt = {}
        for e in self.ENGS:
            cnt = 0
            for op in self.ops[e]:
                if op.dma:
                    if op.key not in keysem:
                        keysem[op.key] = es.enter_context(nc.semaphore("k%d" % len(keysem)))
                        keycnt[op.key] = 0
                    keycnt[op.key] += 16 * op.nd
                    op.tok = (keysem[op.key], keycnt[op.key])
                elif op.signal:
                    cnt += 1
                    op.tok = (engsem[e], cnt)
        self.nsem = len(keysem) + 5
        finals = self.finals

        def mk(ename):
            def body(e):
                waited = {}
                for op in self.ops[ename]:
                    need = {}
                    for d in op.deps:
                        sem, v = d.tok
                        if need.get(sem, (None, 0))[1] < v:
                            need[sem] = (sem, v)
                    for sem, v in need.values():
                        if waited.get(sem, 0) < v:
                            e.wait_ge(sem, v)
                            waited[sem] = v
                    res = op.fn(e)
                    if op.dma:
                        assert len(res) == op.nd, (len(res), op.nd)
                        sem = op.tok[0]
                        for ins in res:
                            ins.then_inc(sem, 16)
                    elif op.signal:
                        res.then_inc(engsem[ename], 1)
                for op in finals:
                    if op.eng == ename:
                        sem, v = op.tok
                        if waited.get(sem, 0) < v:
                            e.wait_ge(sem, v)
                            waited[sem] = v
            return body

        for ename in self.ENGS:
            getattr(block, ename)(mk(ename))


def build_program(S, stages=None, debug_out=False):
    if stages is None:
        stages = ["f00", "m0", "f01", "f10", "m1", "f11"]
    NT = S // TT
    NB = S // 128
    L = weight_layout()
    NW = L.n
    nc = bass.Bass("TRN2", target_bir_lowering=False)
    xT_d = nc.dram_tensor("xT", [D, S], F32, kind="ExternalInput").ap()
    wall_d = nc.dram_tensor("wall", [128, NW], F32, kind="ExternalInput").ap()
    const_d = nc.dram_tensor("consts", [128, NCONST], F32, kind="ExternalInput").ap()
    cos_d = nc.dram_tensor("cos", [128, NB, 64], F32, kind="ExternalInput").ap()
    sin_d = nc.dram_tensor("sin", [128, NB, 64], F32, kind="ExternalInput").ap()
    outT_d = nc.dram_tensor("outT", [D, S], F32, kind="ExternalOutput").ap()
    wbf_d = nc.dram_tensor("wbf", [128, NW], BF16).ap()
    kth_d = nc.dram_tensor("kth", [8, 128, S], BF16).ap()
    vh_d = nc.dram_tensor("vh", [8, 128, NB, 128], BF16).ap()

    es = contextlib.ExitStack()
    sch = Sched()
    A = sch.add
    with es:
        def sb(name, shape, dt):
            return es.enter_context(nc.sbuf_tensor(name, shape, dt))

        x_ts = [sb("x_t%d" % k, [128, KC, TT], F32) for k in range(2)]
        xBs = [[Buf("x%d_%d" % (k, i)) for i in range(KC)] for k in range(2)]
        xcur = [0]
        f_t = sb("f_t", [128, KC, TT], F32)
        fB = [Buf("f%d" % i) for i in range(KC)]
        NF = 16
        fs_t = sb("fs_t", [128, NF, TT], F32)
        fsB = [Buf("fs%d" % i) for i in range(NF)]
        NBS = 48
        bs_t = sb("bs_t", [128, NBS, TT], BF16)
        bsB = [Buf("bs%d" % i) for i in range(NBS)]
        h_t = sb("h_t", [128, KC, TT], BF16)
        hB = [Buf("h%d" % i) for i in range(KC)]
        y_t = sb("y_t", [128, KC, TT], BF16)
        yB = [Buf("y%d" % i) for i in range(KC)]
        NW4 = 4
        w4_t = sb("w4_t", [128, NW4, 2048], BF16)
        w4B = [Buf("w4_%d" % i) for i in range(NW4)]
        NWD = 2
        wd_t = sb("wd_t", [128, NWD, FC * 128], BF16)
        wdB = [Buf("wd_%d" % i) for i in range(NWD)]
        cf = sb("constf", [128, NCONST], F32)
        cfB = Buf("constf")
        gh_t = sb("ghalf", [128, 96], F32)
        cb_t = sb("constb", [128, 8, 128], BF16)
        wst_t = sb("wstb", [128, 4, 128], BF16)
        bsr_t = sb("bsrow", [1, 2, 512], BF16)
        bsf_t = sb("bsrowf", [1, 512], F32)
        setupB = Buf("setup")
        st_f = sb("state_f", [128, 4, 128], F32)
        st_b = sb("state_b", [128, 4, 128], BF16)
        stfB = [Buf("stf%d" % i) for i in range(4)]
        stbB = [Buf("stb%d" % i) for i in range(4)]
        rope_t = sb("rope", [128, 2, 4, 64], F32)
        ropeB = Buf("rope")
        sm_t = sb("small", [128, 64], F32)
        smB = [Buf("sm%d" % i) for i in range(16)]

        psbig = es.enter_context(nc.psum_tensor("psbig", [128, 7, 512], F32))
        ps = [psbig[:, i, :] for i in range(7)]
        psB = [Buf("ps%d" % i, excl=True) for i in range(7)]
        pst = es.enter_context(nc.psum_tensor("pst", [128, 8, 128], BF16))
        pstB = Buf("pst", excl=True)

        def pq(b, c0=0, c1=512):
            return [psB[b]]

        def cc(name, a=0, b=None):
            o, n = CONST_COLS[name]
            if b is None:
                b = n
            return cf[:, o + a:o + b]

        def rstd_op(out_ap, in_ap, scale, r, w, div_ok=False):
            if div_ok and USE_DIV:
                A("scalar", lambda e: e.activation(out=out_ap, in_=in_ap, func=AF.Sqrt, bias=EPS, scale=scale), r=r, w=w)
                return ALU.divide
            if USE_LNEXP:
                A("scalar", lambda e: e.activation(out=out_ap, in_=in_ap, func=AF.Ln, bias=EPS, scale=scale), r=r, w=w)
                A("scalar", lambda e: e.activation(out=out_ap, in_=out_ap, func=AF.Exp, scale=-0.5), r=w, w=w)
            elif USE_ARS:
                A("scalar", lambda e: e.activation(out=out_ap, in_=in_ap, func=AF.Abs_reciprocal_sqrt, bias=EPS, scale=scale),
                  r=r, w=w)
            else:
                A("scalar", lambda e: e.activation(out=out_ap, in_=in_ap, func=AF.Sqrt, bias=EPS, scale=scale), r=r, w=w)
                if FAST_RECIP:
                    A("vector", lambda e: e.reciprocal_approx_fast(out=out_ap, in_=out_ap), r=w, w=w)
                else:
                    A("vector", lambda e: e.reciprocal(out=out_ap, in_=out_ap), r=w, w=w)
            return ALU.mult

        IDb = cb_t[:, 0, :]
        ONESb = cb_t[:, 1, :]
        NEGUb = cb_t[:, 2, :]
        NEGONESb = cb_t[:, 3, :]
        MBb = cb_t[:, 4, :]
        ZEROb = cb_t[:, 5, :]

        def gcol(l, n, dc, half=False):
            j = (l * 6 + n) * 8 + dc
            if half:
                return gh_t[:, j:j + 1]
            o, _ = CONST_COLS["G"]
            return cf[:, o + j:o + j + 1]

        A("gpsimd", lambda e: [e.dma_start(out=cf[:], in_=const_d)], w=[cfB], key="const")
        wsecB = {}
        stage_secs = {"f00": [("gu", 0, 0), ("dn", 0, 0)], "m0": [("in_u", 0), ("in_g", 0), ("in_tok", 0), ("out", 0)],
                      "f01": [("gu", 0, 1), ("dn", 0, 1)], "f10": [("gu", 1, 0), ("dn", 1, 0)],
                      "m1": [("q", 1), ("k", 1), ("v", 1), ("out", 1)], "f11": [("gu", 1, 1), ("dn", 1, 1)]}
        for stg in stages:
            for name in stage_secs[stg]:
                o, n = L.off[name]
                parts = 4 if n > 30000 else (2 if n > 15000 else 1)
                step = n // parts
                bufs = []
                for pi in range(parts):
                    b = Buf("wsec")
                    bufs.append((o + pi * step, o + (pi + 1) * step, b))
                    A("gpsimd", (lambda a0, a1: (lambda e: [e.dma_start(out=wbf_d[:, a0:a1], in_=wall_d[:, a0:a1])]))(
                        o + pi * step, o + (pi + 1) * step), w=[b], key=("cast", name, pi))
                wsecB[name] = bufs

        def wsec_bufs(name, a0, a1):
            return [b for (s0, s1, b) in wsecB[name] if not (a1 <= s0 or a0 >= s1)]

        A("vector", lambda e: e.tensor_scalar(out=gh_t[:], in0=cc("G"), scalar1=0.5, scalar2=0.0, op0=ALU.mult, op1=ALU.add),
          r=[cfB], w=[setupB])
        A("vector", lambda e: e.tensor_copy(out=IDb, in_=cc("ID")), r=[cfB], w=[setupB])
        A("vector", lambda e: e.memset(ONESb, 1.0), w=[setupB])
        A("vector", lambda e: e.tensor_copy(out=NEGUb, in_=cc("NEGU")), r=[cfB], w=[setupB])
        A("vector", lambda e: e.memset(NEGONESb, -1.0), w=[setupB])
        A("vector", lambda e: e.tensor_copy(out=MBb, in_=cc("MB")), r=[cfB], w=[setupB])
        A("vector", lambda e: e.memset(ZEROb, 0.0), w=[setupB])
        for g in range(4):
            A("vector", (lambda g: (lambda e: e.tensor_tensor(out=wst_t[:, g, :], in0=cc("WST", g * 128, g * 128 + 128),
                                                             in1=cc("M01"), op=ALU.mult)))(g), r=[cfB], w=[setupB])
        o_bs, _ = CONST_COLS["BS"]
        A("vector", lambda e: e.tensor_copy(out=bsr_t[0:1, 0, :], in_=cf[0:1, o_bs:o_bs + 512]), r=[cfB], w=[setupB])
        A("vector", lambda e: e.tensor_copy(out=bsf_t[0:1, :], in_=bsr_t[0:1, 0, :]), r=[setupB], w=[setupB])
        A("vector", lambda e: e.tensor_tensor(out=bsr_t[0:1, 1, :], in0=cf[0:1, o_bs:o_bs + 512], in1=bsf_t[0:1, :],
                                              op=ALU.subtract), r=[cfB, setupB], w=[setupB])
        A("vector", lambda e: e.memset(st_f[:], 0.0), w=stfB)
        A("vector", lambda e: e.memset(st_b[:], 0.0), w=stbB + [setupB])

        w4i = [0]
        wdi = [0]

        def load_w4(name, a0, n):
            o, _ = L.off[name]
            slot = w4i[0] % NW4
            w4i[0] += 1
            A("sync", lambda e: [e.dma_start(out=w4_t[:, slot, 0:n], in_=wbf_d[:, o + a0:o + a0 + n])],
              r=wsec_bufs(name, o + a0, o + a0 + n), w=[w4B[slot]], key=("w4", slot))
            return slot

        def load_wd(name, a0):
            o, _ = L.off[name]
            n = FC * 128
            slot = wdi[0] % NWD
            wdi[0] += 1
            A("sync", lambda e: [e.dma_start(out=wd_t[:, slot, :], in_=wbf_d[:, o + a0:o + a0 + n])],
              r=wsec_bufs(name, o + a0, o + a0 + n), w=[wdB[slot]], key=("wd", slot))
            return slot

        fsi = [0]

        def fs_next():
            i = fsi[0] % 4
            fsi[0] += 1
            return i

        SQ = [NBS - 2, NBS - 1]
        sqi = [0]

        def norm_stats(srcs, nfeat, out_slab):
            n = len(srcs)
            for i, (ap, bufs) in enumerate(srcs):
                s = SQ[sqi[0] % 2]
                sqi[0] += 1
                A("scalar", (lambda ap, s: (lambda e: e.activation(out=bs_t[:, s, :], in_=ap, func=AF.Square)))(ap, s),
                  r=bufs, w=[bsB[s]])
                A("tensor", (lambda s, i: (lambda e: e.matmul(ps[6][:, :], ONESb, bs_t[:, s, :], start=(i == 0),
                                                             stop=(i == n - 1))))(s, i),
                  r=[bsB[s], setupB], w=pq(6))
            return rstd_op(fs_t[:, out_slab, :], ps[6][:, :], 1.0 / nfeat, pq(6), [fsB[out_slab]], div_ok=True)

        def prenorm(l, n):
            R = 4
            x_t, xB = x_ts[xcur[0]], xBs[xcur[0]]
            op1 = norm_stats([(x_t[:, dc, :], [xB[dc]]) for dc in range(KC)], D, R)
            for dc in range(KC):
                A("vector", (lambda dc: (lambda e: e.scalar_tensor_tensor(
                    out=h_t[:, dc, :], in0=x_t[:, dc, :], scalar=gcol(l, n, dc), in1=fs_t[:, R, :],
                    op0=ALU.mult, op1=op1)))(dc), r=[xB[dc], fsB[R], cfB], w=[hB[dc]])

        def postnorm_residual(produce, l, n, half):
            R = 5
            x_t, xB = x_ts[xcur[0]], xBs[xcur[0]]
            pend = None
            for dc in range(KC):
                bank = 4 + (dc % 2)
                produce(dc, bank)
                A("scalar", (lambda dc, bank: (lambda e: e.activation(out=f_t[:, dc, :], in_=ps[bank][:, :],
                                                                      func=AF.Copy, scale=gcol(l, n, dc, half))))(dc, bank),
                  r=pq(bank) + [cfB, setupB], w=[fB[dc]])
                s = SQ[sqi[0] % 2]
                sqi[0] += 1
                A("scalar", (lambda bank, s: (lambda e: e.activation(out=bs_t[:, s, :], in_=ps[bank][:, :],
                                                                     func=AF.Square)))(bank, s),
                  r=pq(bank), w=[bsB[s]])
                if pend is not None:
                    pend()
                pend = (lambda s, dc: (lambda: A("tensor", lambda e: e.matmul(
                    ps[6][:, :], ONESb, bs_t[:, s, :], start=(dc == 0), stop=(dc == KC - 1)),
                    r=[bsB[s], setupB], w=pq(6))))(s, dc)
            pend()
            rstd_op(fs_t[:, R, :], ps[6][:, :], 1.0 / D, pq(6), [fsB[R]])
            for dc in range(KC):
                t = 6 + (dc % 2)
                A("vector", (lambda dc, t: (lambda e: e.tensor_tensor(out=fs_t[:, t, :], in0=f_t[:, dc, :],
                                                                     in1=fs_t[:, R, :], op=ALU.mult)))(dc, t),
                  r=[fB[dc], fsB[R]], w=[fsB[t]])
                eng = "vector" if dc in (0, 3, 6) else "gpsimd"
                A(eng, (lambda dc, t: (lambda e: e.tensor_tensor(out=x_t[:, dc, :], in0=x_t[:, dc, :],
                                                                 in1=fs_t[:, t, :], op=ALU.add)))(dc, t),
                  r=[xB[dc], fsB[t]], w=[xB[dc]])

        def ffn(l, f):
            prenorm(l, 0 if f == 0 else 4)
            AT = list(range(0, FC))
            for j in range(FC):
                slot = load_w4(("gu", l, f), j * 2048, 2048)
                gb = j % 2
                ub = 2 + (j % 2)
                A("tensor", (lambda slot, gb: (lambda e: [e.matmul(ps[gb][:, :], w4_t[:, slot, kc * 128:(kc + 1) * 128],
                                                                  h_t[:, kc, :], start=(kc == 0), stop=(kc == KC - 1))
                                                         for kc in range(KC)][-1]))(slot, gb),
                  r=[w4B[slot]] + hB, w=pq(gb))
                A("tensor", (lambda slot, ub: (lambda e: [e.matmul(ps[ub][:, :],
                                                                  w4_t[:, slot, (KC + kc) * 128:(KC + kc + 1) * 128],
                                                                  h_t[:, kc, :], start=(kc == 0), stop=(kc == KC - 1))
                                                         for kc in range(KC)][-1]))(slot, ub),
                  r=[w4B[slot]] + hB, w=pq(ub))
                t = fs_next()
                A("scalar", (lambda gb, t: (lambda e: e.activation(out=fs_t[:, t, :], in_=ps[gb][:, :], func=AF.Silu)))(gb, t),
                  r=pq(gb), w=[fsB[t]])
                A("vector", (lambda ub, t, j: (lambda e: e.tensor_tensor(out=bs_t[:, AT[j], :], in0=fs_t[:, t, :],
                                                                        in1=ps[ub][:, :], op=ALU.mult)))(ub, t, j),
                  r=[fsB[t]] + pq(ub), w=[bsB[AT[j]]])

            def produce(dc, bank):
                slot = load_wd(("dn", l, f), dc * FC * 128)
                A("tensor", lambda e: [e.matmul(ps[bank][:, :], wd_t[:, slot, j * 128:(j + 1) * 128], bs_t[:, AT[j], :],
                                                start=(j == 0), stop=(j == FC - 1)) for j in range(FC)][-1],
                  r=[wdB[slot]] + [bsB[AT[j]] for j in range(FC)], w=pq(bank))

            postnorm_residual(produce, l, 1 if f == 0 else 5, True)

        def outproj(name, l):
            def produce(dc, bank):
                if dc % 2 == 0:
                    produce.slot = load_w4(name, dc * 1024, 2048)
                slot = produce.slot
                base = (dc % 2) * 1024
                A("tensor", lambda e: [e.matmul(ps[bank][:, :], w4_t[:, slot, base + kc * 128:base + (kc + 1) * 128],
                                                y_t[:, kc, :], start=(kc == 0), stop=(kc == KC - 1))
                                       for kc in range(KC)][-1],
                  r=[w4B[slot]] + yB, w=pq(bank))
            postnorm_residual(produce, l, 3, False)

        def mixer0(ti):
            prenorm(0, 2)
            A("gpsimd", lambda e: [e.dma_start(out=rope_t[:, 0, :, :], in_=cos_d[:, ti * 4:ti * 4 + 4, :]),
                                   e.dma_start(out=rope_t[:, 1, :, :], in_=sin_d[:, ti * 4:ti * 4 + 4, :])],
              w=[ropeB], key="rope", nd=2)
            UT = [0, 1, 2, 3]
            GT = [4, 5, 6, 7]
            VLN = [8, 9, 10, 11]
            KZ = [16, 17, 18, 19]
            VR = [20, 21, 22, 23]
            QT = [24, 25, 26, 27]
            QX = [28, 29, 30, 31]
            KT = [32, 33, 34, 35]
            SC = 36
            QR = [12, 13, 37, 38]
            KR = [14, 15, 39, 40]
            VTOK = [8, 9, 10, 11]
            QTOK = [12, 13, 14, 15]
            VT = [(fs_t[:, VTOK[b], :], fsB[VTOK[b]]) for b in range(4)]
            QTK = [(fs_t[:, QTOK[b], :], fsB[QTOK[b]]) for b in range(4)]
            KTK = [(f_t[:, b, :], fB[b]) for b in range(4)]
            PT = [(f_t[:, 4 + i, :], fB[4 + i]) for i in range(4)]
            pt_i = [0]
            bank_i = [0]

            def proj_ug():
                for (name, dst, fn) in ((("in_u", 0), UT, AF.Gelu_apprx_tanh), (("in_g", 0), GT, AF.Silu)):
                    for c in range(4):
                        if c % 2 == 0:
                            slot = load_w4(name, c * 1024, 2048)
                        base = (c % 2) * 1024
                        bank = bank_i[0] % 2
                        bank_i[0] += 1
                        A("tensor", (lambda slot, base, bank: (lambda e: [e.matmul(
                            ps[bank][:, :], w4_t[:, slot, base + kc * 128:base + (kc + 1) * 128], h_t[:, kc, :],
                            start=(kc == 0), stop=(kc == KC - 1)) for kc in range(KC)][-1]))(slot, base, bank),
                          r=[w4B[slot]] + hB, w=pq(bank))
                        A("scalar", (lambda bank, d, fn: (lambda e: e.activation(out=bs_t[:, d, :], in_=ps[bank][:, :],
                                                                                func=fn)))(bank, dst[c], fn),
                          r=pq(bank), w=[bsB[dst[c]]])

            hb_i = [0]

            def tokproj(g8, evac):
                slot = load_w4(("in_tok", 0), g8 * 2048, 2048)
                for b in range(4):
                    hb = hb_i[0] % 4
                    hb_i[0] += 1
                    bank, c0 = 2 + hb // 2, (hb % 2) * 256
                    A("tensor", (lambda slot, b, bank, c0: (lambda e: [e.matmul(
                        ps[bank][:, c0:c0 + 256], h_t[:, kc, b * 128:(b + 1) * 128], w4_t[:, slot, kc * 256:(kc + 1) * 256],
                        start=(kc == 0), stop=(kc == KC - 1)) for kc in range(KC)][-1]))(slot, b, bank, c0),
                      r=[w4B[slot]] + hB, w=pq(bank, c0, c0 + 256))
                    evac(b, ps[bank][:, c0:c0 + 256], pq(bank, c0, c0 + 256))

            def proj_tok_f32(g0, dst, fn):
                for g8 in (g0, g0 + 1):
                    c0 = (g8 % 2) * 256
                    tokproj(g8, lambda b, src, sb_, c0=c0: A("scalar", lambda e: e.activation(
                        out=dst[b][0][:, c0:c0 + 256], in_=src, func=fn), r=sb_, w=[dst[b][1]]))

            def ln_v():
                for b in range(4):
                    X = fs_t[:, VTOK[b], :]
                    s1, nm, s2, rs = sm_t[:, 4 * b:4 * b + 1], sm_t[:, 4 * b + 1:4 * b + 2], sm_t[:, 4 * b + 2:4 * b + 3], sm_t[:, 4 * b + 3:4 * b + 4]
                    mB = smB[b]
                    t0 = fs_next()
                    t1 = fs_next()
                    A("vector", (lambda X, s1: (lambda e: e.tensor_reduce(out=s1, in_=X, axis=AX.X, op=ALU.add)))(X, s1),
                      r=[fsB[VTOK[b]]], w=[mB])
                    A("vector", (lambda s1, nm: (lambda e: e.tensor_scalar(out=nm, in0=s1, scalar1=-1.0 / 512, scalar2=0.0,
                                                                          op0=ALU.mult, op1=ALU.add)))(s1, nm), r=[mB], w=[mB])
                    A("scalar", (lambda X, nm, t0: (lambda e: e.activation(out=fs_t[:, t0, :], in_=X, func=AF.Identity,
                                                                           bias=nm, scale=1.0)))(X, nm, t0),
                      r=[fsB[VTOK[b]], mB], w=[fsB[t0]])
                    A("scalar", (lambda t0, t1: (lambda e: e.activation(out=fs_t[:, t1, :], in_=fs_t[:, t0, :],
                                                                        func=AF.Square)))(t0, t1), r=[fsB[t0]], w=[fsB[t1]])
                    A("vector", (lambda t1, s2: (lambda e: e.tensor_reduce(out=s2, in_=fs_t[:, t1, :], axis=AX.X,
                                                                          op=ALU.add)))(t1, s2), r=[fsB[t1]], w=[mB])
                    rstd_op(rs, s2, 1.0 / 512, [mB], [mB])
                    A("vector", (lambda t0, t1, rs: (lambda e: e.scalar_tensor_tensor(
                        out=fs_t[:, t1, :], in0=fs_t[:, t0, :], scalar=rs, in1=cc("LNG"), op0=ALU.mult, op1=ALU.mult)))(t0, t1, rs),
                      r=[fsB[t0], mB, cfB], w=[fsB[t1]])
                    A("vector", (lambda t1, b: (lambda e: e.tensor_tensor(out=bs_t[:, VLN[b], :], in0=fs_t[:, t1, :],
                                                                         in1=cc("LNB"), op=ALU.add)))(t1, b),
                      r=[fsB[t1], cfB], w=[bsB[VLN[b]]])

            def spatial():
                for g in range(4):
                    bank = 4 + g % 2

                    def spat(e, g=g, bank=bank):
                        last = None
                        for b in range(4):
                            o_ = ps[bank][:, b * 128:(b + 1) * 128]
                            e.matmul(o_, bs_t[:, VLN[b], g * 128:(g + 1) * 128], wst_t[:, g, :], start=True, stop=False)
                            e.matmul(o_, ONESb[0:1, :], bsr_t[0:1, 0, g * 128:(g + 1) * 128], start=False, stop=False)
                            last = e.matmul(o_, ONESb[0:1, :], bsr_t[0:1, 1, g * 128:(g + 1) * 128], start=False, stop=True)
                        return last
                    A("tensor", spat, r=[bsB[VLN[b]] for b in range(4)] + [setupB], w=pq(bank))
                    A("vector", (lambda g, bank: (lambda e: e.tensor_tensor(out=y_t[:, g, :], in0=ps[bank][:, :],
                                                                           in1=bs_t[:, UT[g], :], op=ALU.mult)))(g, bank),
                      r=pq(bank) + [bsB[UT[g]]], w=[yB[g]])

            def rotary(eng, src, b, dst_slab):
                src_ap, src_buf = src
                Xv = src_ap.rearrange("p (h t j) -> p h t j", h=4, t=2)
                Ov = bs_t[:, dst_slab, :].rearrange("p (h t j) -> p h t j", h=4, t=2)
                x1, x2 = Xv[:, :, 0, :], Xv[:, :, 1, :]
                cosb = rope_t[:, 0, b, :].unsqueeze(1).broadcast_to([128, 4, 64])
                sinb = rope_t[:, 1, b, :].unsqueeze(1).broadcast_to([128, 4, 64])
                if eng == "gpsimd":
                    (ta_ap, taB), (tb_ap, tbB) = PT[pt_i[0] % 4], PT[(pt_i[0] + 1) % 4]
                    pt_i[0] += 2
                else:
                    ta = fs_next()
                    tb = fs_next()
                    (ta_ap, taB), (tb_ap, tbB) = (fs_t[:, ta, :], fsB[ta]), (fs_t[:, tb, :], fsB[tb])
                Ta = ta_ap.rearrange("p (u h j) -> p u h j", u=2, h=4)
                Tb = tb_ap.rearrange("p (u h j) -> p u h j", u=2, h=4)
                rd = [src_buf, ropeB]
                A(eng, lambda e: e.tensor_tensor(out=Ta[:, 0], in0=x1, in1=cosb, op=ALU.mult), r=rd, w=[taB])
                A(eng, lambda e: e.tensor_tensor(out=Ta[:, 1], in0=x2, in1=sinb, op=ALU.mult), r=rd, w=[taB])
                A(eng, lambda e: e.tensor_tensor(out=Ov[:, :, 0, :], in0=Ta[:, 0], in1=Ta[:, 1], op=ALU.subtract),
                  r=[taB], w=[bsB[dst_slab]])
                A(eng, lambda e: e.tensor_tensor(out=Tb[:, 0], in0=x1, in1=sinb, op=ALU.mult), r=rd, w=[tbB])
                A(eng, lambda e: e.tensor_tensor(out=Tb[:, 1], in0=x2, in1=cosb, op=ALU.mult), r=rd, w=[tbB])
                A(eng, lambda e: e.tensor_tensor(out=Ov[:, :, 1, :], in0=Tb[:, 0], in1=Tb[:, 1], op=ALU.add),
                  r=[tbB], w=[bsB[dst_slab]])

            def transposes(src_slab, b, dsts):
                A("tensor", lambda e: [e.transpose(pst[:, h, :], bs_t[:, src_slab, h * 128:(h + 1) * 128], IDb)
                                       for h in range(4)][-1], r=[bsB[src_slab], setupB], w=[pstB])
                for (d, mul) in dsts:
                    outap = bs_t[:, d[0]:d[0] + 4, b * 128:(b + 1) * 128]
                    if mul is None:
                        A("scalar", (lambda outap: (lambda e: e.activation(out=outap, in_=pst[:, 0:4, :], func=AF.Copy)))(outap),
                          r=[pstB], w=[bsB[x] for x in d])
                    else:
                        A("vector", (lambda outap: (lambda e: e.tensor_tensor(
                            out=outap, in0=pst[:, 0:4, :], in1=cc("XIBC").rearrange("p (h n) -> p h n", h=4),
                            op=ALU.mult)))(outap), r=[pstB, cfB], w=[bsB[x] for x in d])

            proj_tok_f32(4, KTK, AF.Copy)
            proj_tok_f32(2, QTK, AF.Copy)
            proj_tok_f32(0, VT, AF.Gelu_apprx_tanh)
            for g8 in (6, 7):
                c0 = (g8 % 2) * 256
                tokproj(g8, lambda b, src, sb_, c0=c0: A("scalar", lambda e: e.activation(
                    out=bs_t[:, VR[b], c0:c0 + 256], in_=src, func=AF.Copy), r=sb_, w=[bsB[VR[b]]]))
            proj_ug()
            for b in range(4):
                rotary("gpsimd", KTK[b], b, KR[b])
                for h in range(4):
                    A("gpsimd", (lambda h, b: (lambda e: e.tensor_scalar(
                        out=bs_t[:, KZ[b], h * 128:(h + 1) * 128], in0=bs_t[:, KR[b], h * 128:(h + 1) * 128],
                        scalar1=cc("ZETA", h, h + 1), scalar2=0.0, op0=ALU.mult, op1=ALU.add)))(h, b),
                      r=[bsB[KR[b]], cfB], w=[bsB[KZ[b]]])
            for b in range(4):
                rotary("vector", QTK[b], b, QR[b])
            ln_v()
            for b in range(4):
                transposes(QR[b], b, [(QT, None), (QX, True)])
            for b in range(4):
                transposes(KR[b], b, [(KT, None)])
            spatial()
            sc_i = [0]
            for b in range(4):
                for h in range(4):
                    q = sc_i[0] % 2
                    sc_i[0] += 1
                    hs = slice(h * 128, (h + 1) * 128)
                    bsl = slice(b * 128, (b + 1) * 128)
                    A("tensor", (lambda h, q, bsl: (lambda e: e.matmul(ps[q][:, 0:128], bs_t[:, KT[h], bsl],
                                                                      bs_t[:, QT[h], bsl], start=True, stop=True)))(h, q, bsl),
                      r=[bsB[KT[h]], bsB[QT[h]]], w=[psB[q]])
                    scap = bs_t[:, SC, q * 128:(q + 1) * 128]
                    A("vector", (lambda h, q, scap: (lambda e: e.tensor_tensor(
                        out=scap, in0=ps[q][:, 0:128], in1=cc("DECT", h * 128, h * 128 + 128),
                        op=ALU.mult)))(h, q, scap), r=[psB[q], cfB], w=[scB[q]])
                    ob = 2 + h

                    def core(e, h=h, b=b, hs=hs, bsl=bsl, scap=scap, ob=ob):
                        e.matmul(ps[ob][:, bsl], bs_t[:, VR[b], hs], scap, start=True, stop=False)
                        return e.matmul(ps[ob][:, bsl], st_b[:, h, :], bs_t[:, QX[h], bsl], start=False, stop=True)
                    A("tensor", core, r=[bsB[VR[b]], scB[q], stbB[h], bsB[QX[h]]], w=[psB[ob]])
                    A("tensor", (lambda h, hs, b: (lambda e: e.matmul(ps[6][:, 0:128], bs_t[:, KZ[b], hs],
                                                                     bs_t[:, VR[b], hs], start=True, stop=True)))(h, hs, b),
                      r=[bsB[KZ[b]], bsB[VR[b]]], w=[psB[6]])
                    A("vector", (lambda h: (lambda e: e.scalar_tensor_tensor(
                        out=st_f[:, h, :], in0=st_f[:, h, :], scalar=cc("GAMC", h, h + 1), in1=ps[6][:, 0:128],
                        op0=ALU.mult, op1=ALU.add)))(h), r=[stfB[h], psB[6], cfB], w=[stfB[h]])
                    A("gpsimd", (lambda h: (lambda e: e.tensor_copy(out=st_b[:, h, :], in_=st_f[:, h, :])))(h),
                      r=[stfB[h]], w=[stbB[h]])
            for h in range(4):
                ob = 2 + h
                s = SQ[sqi[0] % 2]
                sqi[0] += 1
                R = 4 + (h % 2)
                gb = h % 2
                t = fs_next()
                A("scalar", (lambda ob, s: (lambda e: e.activation(out=bs_t[:, s, :], in_=ps[ob][:, :], func=AF.Square)))(ob, s),
                  r=pq(ob), w=[bsB[s]])
                A("tensor", (lambda s, gb: (lambda e: e.matmul(ps[gb][:, :], ONESb, bs_t[:, s, :], start=True, stop=True)))(s, gb),
                  r=[bsB[s], setupB], w=pq(gb))
                rstd_op(fs_t[:, R, :], ps[gb][:, :], 1.0 / 128, pq(gb), [fsB[R]])
                A("vector", (lambda ob, h, t, R: (lambda e: e.scalar_tensor_tensor(
                    out=fs_t[:, t, :], in0=ps[ob][:, :], scalar=cc("RETG", h, h + 1), in1=fs_t[:, R, :],
                    op0=ALU.mult, op1=ALU.mult)))(ob, h, t, R), r=pq(ob) + [fsB[R], cfB], w=[fsB[t]])
                A("vector", (lambda h, t: (lambda e: e.tensor_tensor(out=y_t[:, 4 + h, :], in0=fs_t[:, t, :],
                                                                    in1=bs_t[:, GT[h], :], op=ALU.mult)))(h, t),
                  r=[fsB[t], bsB[GT[h]]], w=[yB[4 + h]])
            outproj(("out", 0), 0)

        scB = [Buf("sc%d" % i) for i in range(4)]

        def mixer1(ti):
            prenorm(1, 2)
            QTs = list(range(0, 8))
            KTs = list(range(8, 16))
            VCs = list(range(16, 24))
            KH = [24, 25, 26]
            VH = [27, 28, 29]
            bank_i = [0]
            for (name, dst, scl) in ((("q", 1), QTs, 0.125), (("k", 1), KTs, 1.0)):
                for c in range(8):
                    if c % 2 == 0:
                        slot = load_w4(name, c * 1024, 2048)
                    base = (c % 2) * 1024
                    bank = bank_i[0] % 4
                    bank_i[0] += 1
                    A("tensor", (lambda slot, base, bank: (lambda e: [e.matmul(
                        ps[bank][:, :], w4_t[:, slot, base + kc * 128:base + (kc + 1) * 128], h_t[:, kc, :],
                        start=(kc == 0), stop=(kc == KC - 1)) for kc in range(KC)][-1]))(slot, base, bank),
                      r=[w4B[slot]] + hB, w=pq(bank))
                    A("scalar", (lambda bank, d, scl: (lambda e: e.activation(out=bs_t[:, d, :], in_=ps[bank][:, :],
                                                                             func=AF.Copy, scale=scl)))(bank, dst[c], scl),
                      r=pq(bank), w=[bsB[dst[c]]])
            hb_i = [0]
            for g4 in range(4):
                slot = load_w4(("v", 1), g4 * 2048, 2048)
                for b in range(4):
                    hb = hb_i[0] % 8
                    hb_i[0] += 1
                    bank, c0 = hb // 2, (hb % 2) * 256
                    A("tensor", (lambda slot, b, bank, c0: (lambda e: [e.matmul(
                        ps[bank][:, c0:c0 + 256], h_t[:, kc, b * 128:(b + 1) * 128], w4_t[:, slot, kc * 256:(kc + 1) * 256],
                        start=(kc == 0), stop=(kc == KC - 1)) for kc in range(KC)][-1]))(slot, b, bank, c0),
                      r=[w4B[slot]] + hB, w=pq(bank, c0, c0 + 256))
                    d = VCs[2 * b + g4 // 2]
                    dc0 = (g4 % 2) * 256
                    A("scalar", (lambda bank, c0, d, dc0: (lambda e: e.activation(
                        out=bs_t[:, d, dc0:dc0 + 256], in_=ps[bank][:, c0:c0 + 256], func=AF.Copy)))(bank, c0, d, dc0),
                      r=pq(bank, c0, c0 + 256), w=[bsB[d]])
            if ti < NT - 1:
                A("gpsimd", lambda e: [e.dma_start(out=kth_d[c, :, ti * TT:(ti + 1) * TT], in_=bs_t[:, KTs[c], :])
                                       for c in range(8)],
                  r=[bsB[s] for s in KTs], w=[kvB[ti]], key="kst", nd=8)
                A("gpsimd", lambda e: [e.dma_start(
                    out=vh_d[hf * 4:(hf + 1) * 4, :, ti * 4 + b, :].rearrange("c p f -> p c f"),
                    in_=bs_t[:, VCs[2 * b + hf], :].rearrange("p (c f) -> p c f", c=4))
                    for b in range(4) for hf in range(2)],
                  r=[bsB[s] for s in VCs], w=[kvB2[ti]], key="vst", nd=8)
            kh_i = [0]
            SP0 = 30
            A0 = 36
            SS = [42, 43, 44, 45]
            E0 = 8
            PO = 6

            def attn_pair(c):
                A("tensor", lambda e: e.matmul(ps[PO][:, :], ZEROb, bs_t[:, QTs[c], :], start=True, stop=False,
                                               skip_group_check=True), r=[setupB, bsB[QTs[c]]], w=[psB[PO]])
                for k in (0, 2):
                    A("gpsimd", (lambda k: (lambda e: e.memset(bs_t[:, SS[k]:SS[k] + 2, :], 0.0)))(k),
                      w=[bsB[SS[k]], bsB[SS[k] + 1]])
                units = [(kch, blk) for kch in range(ti, -1, -1) for blk in (3, 2, 1, 0)]
                nU = len(units)
                srcs = {}
                scur = [0]

                def hist_load(kch):
                    sl = kh_i[0] % 3
                    kh_i[0] += 1
                    A("sync", lambda e: [e.dma_start(out=bs_t[:, KH[sl], :], in_=kth_d[c, :, kch * TT:(kch + 1) * TT])],
                      r=[kvB[kch]], w=[bsB[KH[sl]]], key=("kh", sl))
                    A("sync", lambda e: [e.dma_start(out=bs_t[:, VH[sl], :].rearrange("p (b f) -> p b f", b=4),
                                                     in_=vh_d[c, :, kch * 4:(kch + 1) * 4, :])],
                      r=[kvB2[kch]], w=[bsB[VH[sl]]], key=("vh", sl))
                    srcs[kch] = sl

                for kch in range(ti - 1, max(ti - 3, -1), -1):
                    hist_load(kch)

                def geom(ui):
                    kch, blk = units[ui]
                    cur = kch == ti
                    q0 = 128 * blk if cur else 0
                    return kch, blk, cur, q0

                def stage1(ui):
                    kch, blk, cur, q0 = geom(ui)
                    if (not cur) and blk == 3 and kch - 2 >= 0:
                        hist_load(kch - 2)
                    zk = 2 * (ui % 3)
                    ks = KTs[c] if cur else KH[srcs[kch]]

                    def zmm(e):
                        for hh in (0, 1):
                            hp = slice(hh * 64, hh * 64 + 64)
                            r_ = e.matmul(ps[zk + hh][:, q0:512], bs_t[hp, ks, blk * 128:(blk + 1) * 128],
                                          bs_t[hp, QTs[c], q0:512], start=True, stop=False, skip_group_check=True)
                        if cur:
                            for hh in (0, 1):
                                r_ = e.matmul(ps[zk + hh][:, q0:q0 + 128], IDb, MBb, start=False, stop=False,
                                              skip_group_check=True)
                        return r_
                    A("tensor", zmm, r=[bsB[ks], bsB[QTs[c]], setupB], w=[psB[zk], psB[zk + 1]])
                    ee = E0 + 2 * (ui % 3)
                    A("scalar", lambda e: e.activation(out=fs_t[:, ee:ee + 2, q0:512], in_=psbig[:, zk:zk + 2, q0:512],
                                                       func=AF.Exp), r=[psB[zk], psB[zk + 1]], w=[fsB[ee], fsB[ee + 1]])

                def stage1b(ui):
                    kch, blk, cur, q0 = geom(ui)
                    ee = E0 + 2 * (ui % 3)
                    sp = SP0 + 2 * (ui % 3)
                    A("scalar", lambda e: e.activation(out=bs_t[:, sp:sp + 2, q0:512], in_=fs_t[:, ee:ee + 2, q0:512],
                                                       func=AF.Ln, bias=1.0, scale=1.0),
                      r=[fsB[ee], fsB[ee + 1]], w=[bsB[sp], bsB[sp + 1]])

                def stage2(ui):
                    kch, blk, cur, q0 = geom(ui)
                    zk = 2 * (ui % 3)
                    sp = SP0 + 2 * (ui % 3)
                    first = ui == 0
                    sc_ = SS[2 * scur[0]]
                    if cur:
                        sn_ = sc_
                    else:
                        scur[0] ^= 1
                        sn_ = SS[2 * scur[0]]

                    def cmm(e):
                        for hh in (0, 1):
                            r_ = e.matmul(ps[zk + hh][:, q0:512], NEGUb, bs_t[:, sp + hh, q0:512], start=False, stop=first,
                                          skip_group_check=True)
                            if not first:
                                r_ = e.matmul(ps[zk + hh][:, q0:512], NEGONESb, bs_t[:, sc_ + hh, q0:512], start=False,
                                              stop=True, skip_group_check=True)
                        return r_
                    A("tensor", cmm, r=[bsB[sp], bsB[sp + 1], bsB[sc_], bsB[sc_ + 1], setupB], w=[psB[zk], psB[zk + 1]])
                    if ui < nU - 1:
                        A("vector", lambda e: e.tensor_tensor(out=bs_t[:, sn_:sn_ + 2, q0:512], in0=bs_t[:, sc_:sc_ + 2, q0:512],
                                                              in1=bs_t[:, sp:sp + 2, q0:512], op=ALU.add),
                          r=[bsB[sc_], bsB[sc_ + 1], bsB[sp], bsB[sp + 1]], w=[bsB[sn_], bsB[sn_ + 1]])

                def stage2b(ui):
                    kch, blk, cur, q0 = geom(ui)
                    zk = 2 * (ui % 3)
                    a_ = A0 + 2 * (ui % 3)
                    A("scalar", lambda e: e.activation(out=bs_t[:, a_:a_ + 2, q0:512], in_=psbig[:, zk:zk + 2, q0:512],
                                                       func=AF.Exp), r=[psB[zk], psB[zk + 1]], w=[bsB[a_], bsB[a_ + 1]])

                def stage3(ui):
                    kch, blk, cur, q0 = geom(ui)
                    a_ = A0 + 2 * (ui % 3)
                    if cur:
                        vs = VCs[2 * blk + c // 4]
                        vc0 = (c % 4) * 128
                    else:
                        vs = VH[srcs[kch]]
                        vc0 = blk * 128
                    last = ui == nU - 1

                    def avmm(e):
                        for hh in (0, 1):
                            hp = slice(hh * 64, hh * 64 + 64)
                            r_ = e.matmul(ps[PO][hp, q0:512], bs_t[:, vs, vc0 + hh * 64:vc0 + hh * 64 + 64],
                                          bs_t[:, a_ + hh, q0:512], start=False, stop=last, tile_position=(0, hh * 64),
                                          skip_group_check=True)
                        return r_
                    A("tensor", avmm, r=[bsB[vs], bsB[a_], bsB[a_ + 1]], w=[psB[PO]])

                for step in range(nU + 3):
                    if step < nU:
                        stage1(step)
                    if 0 <= step - 1 < nU:
                        stage2(step - 1)
                    if 0 <= step - 2 < nU:
                        stage2b(step - 2)
                    if step < nU:
                        stage1b(step)
                    if 0 <= step - 3 < nU:
                        stage3(step - 3)
                A("vector", lambda e: e.tensor_copy(out=y_t[:, c, :], in_=ps[PO][:, :]), r=[psB[PO]], w=[yB[c]])

            for c in range(8):
                attn_pair(c)
            outproj(("out", 1), 1)

        kvB = [Buf("kv%d" % i) for i in range(NT)]
        kvB2 = [Buf("kvv%d" % i) for i in range(NT)]

        xsrc = xT_d.rearrange("(kc p) s -> p kc s", p=128)
        odst = outT_d.rearrange("(kc p) s -> p kc s", p=128)
        last_store = None
        stores = []

        def load_x(ti):
            k = ti % 2
            A("gpsimd", lambda e: [e.dma_start(out=x_ts[k][:], in_=xsrc[:, :, ti * TT:(ti + 1) * TT])], w=xBs[k], key=("xld", k))

        load_x(0)
        for ti in range(NT):
            xcur[0] = ti % 2
            if ti + 1 < NT:
                load_x(ti + 1)
            for stg in stages:
                if stg == "f00":
                    ffn(0, 0)
                elif stg == "m0":
                    mixer0(ti)
                elif stg == "f01":
                    ffn(0, 1)
                elif stg == "f10":
                    ffn(1, 0)
                elif stg == "m1":
                    mixer1(ti)
                elif stg == "f11":
                    ffn(1, 1)
            last_store = A("gpsimd", (lambda ti: (lambda e: [e.dma_start(out=odst[:, :, ti * TT:(ti + 1) * TT],
                                                                        in_=x_ts[ti % 2][:])]))(ti), r=xBs[ti % 2],
                           key=("xst", ti % 2))
            stores.append(last_store)
        sch.finals.extend(stores[-2:])
        block = es.enter_context(nc.Block())
        sch.emit(nc, block, es)
    return nc, sch


_CACHE = {}


def kernel(**inputs):
    x = np.asarray(inputs["x"], dtype=np.float32)
    B, S, _ = x.shape
    inp = {k: np.asarray(v, dtype=np.float32) for k, v in inputs.items()}
    wall = pack_weights(inp)
    consts = pack_consts(inp)
    cos, sin = rope_tables(S)
    nc, _ = build_program(S)
    in_maps = []
    for b in range(B):
        in_maps.append({"xT": np.ascontiguousarray(x[b].T), "wall": wall, "consts": consts, "cos": cos, "sin": sin})
    res = run_bass_kernel_spmd(nc, in_maps, core_ids=list(range(B)))
    out = np.empty((B, S, D), np.float32)
    for b in range(B):
        out[b] = res.results[b]["outT"].T
    return out
```

```python
import contextlib
import numpy as np
import concourse.bass as bass
import concourse.mybir as mybir
from concourse.bass_utils import run_bass_kernel_spmd

F32 = mybir.dt.float32
BF16 = mybir.dt.bfloat16
AF = mybir.ActivationFunctionType
ALU = mybir.AluOpType
AX = mybir.AxisListType

D = 1024
KC = 8
FF = 2816
FC = 22
TT = 512
EPS = 1e-6
NEG = -30000.0
FAST_RECIP = False
USE_LNEXP = True
USE_DIV = False
USE_ARS = False


def _pack(W, nw):
    K, M = W.shape
    return np.ascontiguousarray(W.reshape(K // 128, 128, M // nw, nw).transpose(1, 2, 0, 3)).reshape(128, -1)


class WLayout:
    def __init__(self):
        self.off = {}
        self.n = 0

    def add(self, name, ncols):
        self.off[name] = (self.n, ncols)
        self.n += ncols


def weight_layout():
    L = WLayout()
    for l in range(2):
        for f in range(2):
            L.add(("gu", l, f), FC * 2 * KC * 128)
            L.add(("dn", l, f), KC * FC * 128)
        if l == 0:
            L.add(("in_u", 0), 4 * KC * 128)
            L.add(("in_g", 0), 4 * KC * 128)
            L.add(("in_tok", 0), 8 * KC * 256)
            L.add(("out", 0), KC * KC * 128)
        else:
            L.add(("q", 1), KC * KC * 128)
            L.add(("k", 1), KC * KC * 128)
            L.add(("v", 1), 4 * KC * 256)
            L.add(("out", 1), KC * KC * 128)
    return L


def pack_weights(inp):
    L = weight_layout()
    wall = np.empty((128, L.n), np.float32)

    def put(name, arr):
        o, n = L.off[name]
        assert arr.shape == (128, n), (name, arr.shape, n)
        wall[:, o:o + n] = arr

    for l in range(2):
        for f in range(2):
            g = _pack(inp["ffn_w_gate"][l, f], 128).reshape(128, FC, 1, KC * 128)
            u = _pack(inp["ffn_w_up"][l, f], 128).reshape(128, FC, 1, KC * 128)
            put(("gu", l, f), np.concatenate([g, u], axis=2).reshape(128, -1))
            put(("dn", l, f), _pack(inp["ffn_w_down"][l, f], 128))
    win = inp["ab_w_in"][0]
    put(("in_u", 0), _pack(win[:, 0:512], 128))
    put(("in_g", 0), _pack(win[:, 2560:3072], 128))
    put(("in_tok", 0), _pack(win[:, 512:2560], 256))
    put(("out", 0), _pack(inp["ab_w_out"][0], 128))
    wq = inp["sb_w_qkv"][0]
    put(("q", 1), _pack(wq[:, 0:1024], 128))
    put(("k", 1), _pack(wq[:, 1024:2048], 128))
    put(("v", 1), _pack(wq[:, 2048:3072], 256))
    put(("out", 1), _pack(inp["sb_w_out"][0], 128))
    return wall


CONST_COLS = {}
_c = 0
for _n, _w in [("G", 96), ("RETG", 4), ("ZETA", 4), ("GAMC", 4), ("LNG", 512), ("LNB", 512), ("BS", 512),
               ("DECT", 512), ("XIBC", 512), ("WST", 512), ("M01", 128), ("MB", 128), ("NEGU", 128),
               ("ID", 128)]:
    CONST_COLS[_n] = (_c, _w)
    _c += _w
NCONST = _c


def pack_consts(inp):
    c = np.zeros((128, NCONST), np.float32)

    def put(name, arr):
        o, n = CONST_COLS[name]
        c[:, o:o + n] = arr

    g = inp["norm_g"].reshape(12, KC, 128)
    put("G", g.transpose(2, 0, 1).reshape(128, 96))
    put("RETG", inp["ret_norm_g"][0].reshape(4, 128).T)
    lg = np.log1p(-(2.0 ** (-5.0 - np.arange(4, dtype=np.float64))))
    idx = np.arange(128, dtype=np.float64)
    scale = 128.0 ** -0.5
    zeta = np.exp(lg[:, None] * (127.0 - idx)) * scale
    put("ZETA", zeta.T.astype(np.float32))
    put("GAMC", np.broadcast_to(np.exp(lg * 128.0)[None, :], (128, 4)).astype(np.float32))
    put("LNG", np.broadcast_to(inp["gmlp_v_norm_g"][0][None, :], (128, 512)))
    put("LNB", np.broadcast_to(inp["gmlp_v_norm_b"][0][None, :], (128, 512)))
    put("BS", np.broadcast_to(inp["gmlp_b_s"][0].reshape(1, 512), (128, 512)))
    diff = idx[:, None] - idx[None, :]
    dec = np.where(diff >= 0, np.exp(lg[:, None, None] * np.maximum(diff, 0.0)), 0.0) * scale
    put("DECT", dec.transpose(2, 0, 1).reshape(128, 512).astype(np.float32))
    xi = np.exp(lg[:, None] * (idx + 1.0))
    put("XIBC", np.broadcast_to(xi.reshape(1, 512), (128, 512)).astype(np.float32))
    put("WST", inp["gmlp_w_s"][0].transpose(2, 0, 1).reshape(128, 512))
    s_i = np.arange(128)[:, None]
    t_i = np.arange(128)[None, :]
    put("M01", (t_i >= s_i).astype(np.float32))
    put("MB", np.where(s_i >= t_i, NEG, 0.0).astype(np.float32))
    put("NEGU", np.where(s_i >= t_i, -1.0, 0.0).astype(np.float32))
    put("ID", np.eye(128, dtype=np.float32))
    return c


def rope_tables(S):
    half = 64
    inv = (10000.0 ** (-np.arange(half, dtype=np.float32) / half)).astype(np.float32)
    ang = np.arange(S, dtype=np.float32)[:, None] * inv[None, :]
    cos = np.cos(ang).astype(np.float32).reshape(S // 128, 128, half).transpose(1, 0, 2)
    sin = np.sin(ang).astype(np.float32).reshape(S // 128, 128, half).transpose(1, 0, 2)
    return np.ascontiguousarray(cos), np.ascontiguousarray(sin)


class Buf:
    __slots__ = ("name", "lw", "rd", "excl")

    def __init__(self, name="", excl=False):
        self.name = name
        self.lw = None
        self.rd = []
        self.excl = excl


class Op:
    __slots__ = ("eng", "fn", "deps", "dma", "key", "nd", "signal", "tok", "idx")


class Sched:
    ENGS = ("sync", "scalar", "vector", "tensor", "gpsimd")

    def __init__(self):
        self.ops = {e: [] for e in self.ENGS}
        self.n = 0
        self.finals = []

    def add(self, eng, fn, r=(), w=(), key=None, nd=1):
        op = Op()
        op.eng, op.fn, op.dma, op.key, op.nd, op.signal, op.tok = eng, fn, key is not None, key, nd, False, None
        op.idx = self.n
        self.n += 1
        if any(b.excl for b in r):
            w = list(w) + [b for b in r if b.excl]
            r = [b for b in r if not b.excl]
        deps = {}
        for b in r:
            if b.lw is not None:
                deps[b.lw.idx] = (b.lw, True)
        for b in w:
            if b.lw is not None and b.lw.idx not in deps:
                deps[b.lw.idx] = (b.lw, False)
            for o in b.rd:
                if o.idx not in deps:
                    deps[o.idx] = (o, False)
        final = []
        for d, raw in deps.values():
            if d is op:
                continue
            if (not d.dma) and (not op.dma) and d.eng == eng and not raw and eng != "gpsimd":
                continue
            final.append(d)
            d.signal = True
        op.deps = final
        for b in r:
            b.rd.append(op)
        for b in w:
            b.lw = op
            b.rd = []
        self.ops[eng].append(op)
        return op

    def emit(self, nc, block, es):
        engsem = {}
        for e in self.ENGS:
            engsem[e] = es.enter_context(nc.semaphore("e_" + e))
        keysem = {}
        keycnt = {}
        for e in self.ENGS:
            cnt = 0
            for op in self.ops[e]:
                if op.dma:
                    if op.key not in keysem:
                        keysem[op.key] = es.enter_context(nc.semaphore("k%d" % len(keysem)))
                        keycnt[op.key] = 0
                    keycnt[op.key] += 16 * op.nd
                    op.tok = (keysem[op.key], keycnt[op.key])
                elif op.signal:
                    cnt += 1
                    op.tok = (engsem[e], cnt)
        self.nsem = len(keysem) + 5
        finals = self.finals

        def mk(ename):
            def body(e):
                waited = {}
                for op in self.ops[ename]:
                    need = {}
                    for d in op.deps:
                        sem, v = d.tok
                        if need.get(sem, (None, 0))[1] < v:
                            need[sem] = (sem, v)
                    for sem, v in need.values():
                        if waited.get(sem, 0) < v:
                            e.wait_ge(sem, v)
                            waited[sem] = v
                    res = op.fn(e)
                    if op.dma:
                        assert len(res) == op.nd, (len(res), op.nd)
                        sem = op.tok[0]
                        for ins in res:
                            ins.then_inc(sem, 16)
                    elif op.signal:
                        res.then_inc(engsem[ename], 1)
                for op in finals:
                    if op.eng == ename:
                        sem, v = op.tok
                        if waited.get(sem, 0) < v:
                            e.wait_ge(sem, v)
                            waited[sem] = v
            return body

        for ename in self.ENGS:
            getattr(block, ename)(mk(ename))


def build_program(S, stages=None, debug_out=False):
    if stages is None:
        stages = ["f00", "m0", "f01", "f10", "m1", "f11"]
    NT = S // TT
    NB = S // 128
    L = weight_layout()
    NW = L.n
    nc = bass.Bass("TRN2", target_bir_lowering=False)
    xT_d = nc.dram_tensor("xT", [D, S], F32, kind="ExternalInput").ap()
    wall_d = nc.dram_tensor("wall", [128, NW], F32, kind="ExternalInput").ap()
    const_d = nc.dram_tensor("consts", [128, NCONST], F32, kind="ExternalInput").ap()
    cos_d = nc.dram_tensor("cos", [128, NB, 64], F32, kind="ExternalInput").ap()
    sin_d = nc.dram_tensor("sin", [128, NB, 64], F32, kind="ExternalInput").ap()
    outT_d = nc.dram_tensor("outT", [D, S], F32, kind="ExternalOutput").ap()
    wbf_d = nc.dram_tensor("wbf", [128, NW], BF16).ap()
    kth_d = nc.dram_tensor("kth", [8, 128, S], BF16).ap()
    vh_d = nc.dram_tensor("vh", [8, 128, NB, 128], BF16).ap()

    es = contextlib.ExitStack()
    sch = Sched()
    A = sch.add
    with es:
        def sb(name, shape, dt):
            return es.enter_context(nc.sbuf_tensor(name, shape, dt))

        x_ts = [sb("x_t%d" % k, [128, KC, TT], F32) for k in range(2)]
        xBs = [[Buf("x%d_%d" % (k, i)) for i in range(KC)] for k in range(2)]
        xcur = [0]
        f_t = sb("f_t", [128, KC, TT], F32)
        fB = [Buf("f%d" % i) for i in range(KC)]
        NF = 16
        fs_t = sb("fs_t", [128, NF, TT], F32)
        fsB = [Buf("fs%d" % i) for i in range(NF)]
        NBS = 48
        bs_t = sb("bs_t", [128, NBS, TT], BF16)
        bsB = [Buf("bs%d" % i) for i in range(NBS)]
        h_t = sb("h_t", [128, KC, TT], BF16)
        hB = [Buf("h%d" % i) for i in range(KC)]
        y_t = sb("y_t", [128, KC, TT], BF16)
        yB = [Buf("y%d" % i) for i in range(KC)]
        NW4 = 4
        w4_t = sb("w4_t", [128, NW4, 2048], BF16)
        w4B = [Buf("w4_%d" % i) for i in range(NW4)]
        NWD = 2
        wd_t = sb("wd_t", [128, NWD, FC * 128], BF16)
        wdB = [Buf("wd_%d" % i) for i in range(NWD)]
        cf = sb("constf", [128, NCONST], F32)
        cfB = Buf("constf")
        gh_t = sb("ghalf", [128, 96], F32)
        cb_t = sb("constb", [128, 8, 128], BF16)
        wst_t = sb("wstb", [128, 4, 128], BF16)
        bsr_t = sb("bsrow", [1, 2, 512], BF16)
        bsf_t = sb("bsrowf", [1, 512], F32)
        setupB = Buf("setup")
        st_f = sb("state_f", [128, 4, 128], F32)
        st_b = sb("state_b", [128, 4, 128], BF16)
        stfB = [Buf("stf%d" % i) for i in range(4)]
        stbB = [Buf("stb%d" % i) for i in range(4)]
        rope_t = sb("rope", [128, 2, 4, 64], F32)
        ropeB = Buf("rope")
        sm_t = sb("small", [128, 64], F32)
        smB = [Buf("sm%d" % i) for i in range(16)]

        psbig = es.enter_context(nc.psum_tensor("psbig", [128, 7, 512], F32))
        ps = [psbig[:, i, :] for i in range(7)]
        psB = [Buf("ps%d" % i, excl=True) for i in range(7)]
        pst = es.enter_context(nc.psum_tensor("pst", [128, 8, 128], BF16))
        pstB = Buf("pst", excl=True)

        def pq(b, c0=0, c1=512):
            return [psB[b]]

        def cc(name, a=0, b=None):
            o, n = CONST_COLS[name]
            if b is None:
                b = n
            return cf[:, o + a:o + b]

        def rstd_op(out_ap, in_ap, scale, r, w, div_ok=False):
            if div_ok and USE_DIV:
                A("scalar", lambda e: e.activation(out=out_ap, in_=in_ap, func=AF.Sqrt, bias=EPS, scale=scale), r=r, w=w)
                return ALU.divide
            if USE_LNEXP:
                A("scalar", lambda e: e.activation(out=out_ap, in_=in_ap, func=AF.Ln, bias=EPS, scale=scale), r=r, w=w)
                A("scalar", lambda e: e.activation(out=out_ap, in_=out_ap, func=AF.Exp, scale=-0.5), r=w, w=w)
            elif USE_ARS:
                A("scalar", lambda e: e.activation(out=out_ap, in_=in_ap, func=AF.Abs_reciprocal_sqrt, bias=EPS, scale=scale),
                  r=r, w=w)
            else:
                A("scalar", lambda e: e.activation(out=out_ap, in_=in_ap, func=AF.Sqrt, bias=EPS, scale=scale), r=r, w=w)
                if FAST_RECIP:
                    A("vector", lambda e: e.reciprocal_approx_fast(out=out_ap, in_=out_ap), r=w, w=w)
                else:
                    A("vector", lambda e: e.reciprocal(out=out_ap, in_=out_ap), r=w, w=w)
            return ALU.mult

        IDb = cb_t[:, 0, :]
        ONESb = cb_t[:, 1, :]
        NEGUb = cb_t[:, 2, :]
        NEGONESb = cb_t[:, 3, :]
        MBb = cb_t[:, 4, :]
        ZEROb = cb_t[:, 5, :]

        def gcol(l, n, dc, half=False):
            j = (l * 6 + n) * 8 + dc
            if half:
                return gh_t[:, j:j + 1]
            o, _ = CONST_COLS["G"]
            return cf[:, o + j:o + j + 1]

        A("gpsimd", lambda e: [e.dma_start(out=cf[:], in_=const_d)], w=[cfB], key="const")
        wsecB = {}
        stage_secs = {"f00": [("gu", 0, 0), ("dn", 0, 0)], "m0": [("in_u", 0), ("in_g", 0), ("in_tok", 0), ("out", 0)],
                      "f01": [("gu", 0, 1), ("dn", 0, 1)], "f10": [("gu", 1, 0), ("dn", 1, 0)],
                      "m1": [("q", 1), ("k", 1), ("v", 1), ("out", 1)], "f11": [("gu", 1, 1), ("dn", 1, 1)]}
        def emit_casts(stg):
            for name in stage_secs[stg]:
                o, n = L.off[name]
                parts = 4 if n > 30000 else (2 if n > 15000 else 1)
                step = n // parts
                bufs = []
                for pi in range(parts):
                    b = Buf("wsec")
                    bufs.append((o + pi * step, o + (pi + 1) * step, b))
                    A("gpsimd", (lambda a0, a1: (lambda e: [e.dma_start(out=wbf_d[:, a0:a1], in_=wall_d[:, a0:a1])]))(
                        o + pi * step, o + (pi + 1) * step), w=[b], key=("cast", name, pi))
                wsecB[name] = bufs

        for stg in stages[:2]:
            emit_casts(stg)

        def wsec_bufs(name, a0, a1):
            return [b for (s0, s1, b) in wsecB[name] if not (a1 <= s0 or a0 >= s1)]

        A("vector", lambda e: e.tensor_scalar(out=gh_t[:], in0=cc("G"), scalar1=0.5, scalar2=0.0, op0=ALU.mult, op1=ALU.add),
          r=[cfB], w=[setupB])
        A("vector", lambda e: e.tensor_copy(out=IDb, in_=cc("ID")), r=[cfB], w=[setupB])
        A("vector", lambda e: e.memset(ONESb, 1.0), w=[setupB])
        A("vector", lambda e: e.tensor_copy(out=NEGUb, in_=cc("NEGU")), r=[cfB], w=[setupB])
        A("vector", lambda e: e.memset(NEGONESb, -1.0), w=[setupB])
        A("vector", lambda e: e.tensor_copy(out=MBb, in_=cc("MB")), r=[cfB], w=[setupB])
        A("vector", lambda e: e.memset(ZEROb, 0.0), w=[setupB])
        for g in range(4):
            A("vector", (lambda g: (lambda e: e.tensor_tensor(out=wst_t[:, g, :], in0=cc("WST", g * 128, g * 128 + 128),
                                                             in1=cc("M01"), op=ALU.mult)))(g), r=[cfB], w=[setupB])
        o_bs, _ = CONST_COLS["BS"]
        A("vector", lambda e: e.tensor_copy(out=bsr_t[0:1, 0, :], in_=cf[0:1, o_bs:o_bs + 512]), r=[cfB], w=[setupB])
        A("vector", lambda e: e.tensor_copy(out=bsf_t[0:1, :], in_=bsr_t[0:1, 0, :]), r=[setupB], w=[setupB])
        A("vector", lambda e: e.tensor_tensor(out=bsr_t[0:1, 1, :], in0=cf[0:1, o_bs:o_bs + 512], in1=bsf_t[0:1, :],
                                              op=ALU.subtract), r=[cfB, setupB], w=[setupB])
        A("vector", lambda e: e.memset(st_f[:], 0.0), w=stfB)
        A("vector", lambda e: e.memset(st_b[:], 0.0), w=stbB + [setupB])

        w4i = [0]
        wdi = [0]

        def load_w4(name, a0, n):
            o, _ = L.off[name]
            slot = w4i[0] % NW4
            w4i[0] += 1
            A("sync", lambda e: [e.dma_start(out=w4_t[:, slot, 0:n], in_=wbf_d[:, o + a0:o + a0 + n])],
              r=wsec_bufs(name, o + a0, o + a0 + n), w=[w4B[slot]], key=("w4", slot))
            return slot

        def load_wd(name, a0):
            o, _ = L.off[name]
            n = FC * 128
            slot = wdi[0] % NWD
            wdi[0] += 1
            A("sync", lambda e: [e.dma_start(out=wd_t[:, slot, :], in_=wbf_d[:, o + a0:o + a0 + n])],
              r=wsec_bufs(name, o + a0, o + a0 + n), w=[wdB[slot]], key=("wd", slot))
            return slot

        fsi = [0]

        def fs_next():
            i = fsi[0] % 4
            fsi[0] += 1
            return i

        SQ = [NBS - 2, NBS - 1]
        sqi = [0]

        def norm_stats(srcs, nfeat, out_slab):
            n = len(srcs)
            for i, (ap, bufs) in enumerate(srcs):
                s = SQ[sqi[0] % 2]
                sqi[0] += 1
                A("scalar", (lambda ap, s: (lambda e: e.activation(out=bs_t[:, s, :], in_=ap, func=AF.Square)))(ap, s),
                  r=bufs, w=[bsB[s]])
                A("tensor", (lambda s, i: (lambda e: e.matmul(ps[6][:, :], ONESb, bs_t[:, s, :], start=(i == 0),
                                                             stop=(i == n - 1))))(s, i),
                  r=[bsB[s], setupB], w=pq(6))
            return rstd_op(fs_t[:, out_slab, :], ps[6][:, :], 1.0 / nfeat, pq(6), [fsB[out_slab]], div_ok=True)

        def prenorm(l, n):
            R = 4
            x_t, xB = x_ts[xcur[0]], xBs[xcur[0]]
            op1 = norm_stats([(x_t[:, dc, :], [xB[dc]]) for dc in range(KC)], D, R)
            for dc in range(KC):
                A("vector", (lambda dc: (lambda e: e.scalar_tensor_tensor(
                    out=h_t[:, dc, :], in0=x_t[:, dc, :], scalar=gcol(l, n, dc), in1=fs_t[:, R, :],
                    op0=ALU.mult, op1=op1)))(dc), r=[xB[dc], fsB[R], cfB], w=[hB[dc]])

        def postnorm_residual(produce, l, n, half):
            R = 5
            x_t, xB = x_ts[xcur[0]], xBs[xcur[0]]
            pend = None
            for dc in range(KC):
                bank = 4 + (dc % 2)
                produce(dc, bank)
                A("scalar", (lambda dc, bank: (lambda e: e.activation(out=f_t[:, dc, :], in_=ps[bank][:, :],
                                                                      func=AF.Copy)))(dc, bank),
                  r=pq(bank), w=[fB[dc]])
                s = SQ[sqi[0] % 2]
                sqi[0] += 1
                A("scalar", (lambda dc, s: (lambda e: e.activation(out=bs_t[:, s, :], in_=f_t[:, dc, :],
                                                                   func=AF.Square)))(dc, s),
                  r=[fB[dc]], w=[bsB[s]])
                if pend is not None:
                    pend()
                pend = (lambda s, dc: (lambda: A("tensor", lambda e: e.matmul(
                    ps[6][:, :], ONESb, bs_t[:, s, :], start=(dc == 0), stop=(dc == KC - 1)),
                    r=[bsB[s], setupB], w=pq(6))))(s, dc)
            pend()
            op1 = rstd_op(fs_t[:, R, :], ps[6][:, :], 1.0 / D, pq(6), [fsB[R]], div_ok=True)
            for dc in range(KC):
                t = 6 + (dc % 2)
                A("vector", (lambda dc, t: (lambda e: e.scalar_tensor_tensor(
                    out=fs_t[:, t, :], in0=f_t[:, dc, :], scalar=gcol(l, n, dc, half), in1=fs_t[:, R, :],
                    op0=ALU.mult, op1=op1)))(dc, t), r=[fB[dc], fsB[R], cfB, setupB], w=[fsB[t]])
                A("gpsimd", (lambda dc, t: (lambda e: e.tensor_tensor(out=x_t[:, dc, :], in0=x_t[:, dc, :],
                                                                      in1=fs_t[:, t, :], op=ALU.add)))(dc, t),
                  r=[xB[dc], fsB[t]], w=[xB[dc]])

        def ffn(l, f):
            prenorm(l, 0 if f == 0 else 4)
            AT = list(range(0, FC))
            for j in range(FC):
                slot = load_w4(("gu", l, f), j * 2048, 2048)
                gb = j % 2
                ub = 2 + (j % 2)
                A("tensor", (lambda slot, gb: (lambda e: [e.matmul(ps[gb][:, :], w4_t[:, slot, kc * 128:(kc + 1) * 128],
                                                                  h_t[:, kc, :], start=(kc == 0), stop=(kc == KC - 1))
                                                         for kc in range(KC)][-1]))(slot, gb),
                  r=[w4B[slot]] + hB, w=pq(gb))
                A("tensor", (lambda slot, ub: (lambda e: [e.matmul(ps[ub][:, :],
                                                                  w4_t[:, slot, (KC + kc) * 128:(KC + kc + 1) * 128],
                                                                  h_t[:, kc, :], start=(kc == 0), stop=(kc == KC - 1))
                                                         for kc in range(KC)][-1]))(slot, ub),
                  r=[w4B[slot]] + hB, w=pq(ub))
                t = fs_next()
                A("scalar", (lambda gb, t: (lambda e: e.activation(out=fs_t[:, t, :], in_=ps[gb][:, :], func=AF.Silu)))(gb, t),
                  r=pq(gb), w=[fsB[t]])
                A("vector", (lambda ub, t, j: (lambda e: e.tensor_tensor(out=bs_t[:, AT[j], :], in0=fs_t[:, t, :],
                                                                        in1=ps[ub][:, :], op=ALU.mult)))(ub, t, j),
                  r=[fsB[t]] + pq(ub), w=[bsB[AT[j]]])

            def produce(dc, bank):
                slot = load_wd(("dn", l, f), dc * FC * 128)
                A("tensor", lambda e: [e.matmul(ps[bank][:, :], wd_t[:, slot, j * 128:(j + 1) * 128], bs_t[:, AT[j], :],
                                                start=(j == 0), stop=(j == FC - 1)) for j in range(FC)][-1],
                  r=[wdB[slot]] + [bsB[AT[j]] for j in range(FC)], w=pq(bank))

            postnorm_residual(produce, l, 1 if f == 0 else 5, True)

        def outproj(name, l):
            def produce(dc, bank):
                if dc % 2 == 0:
                    produce.slot = load_w4(name, dc * 1024, 2048)
                slot = produce.slot
                base = (dc % 2) * 1024
                A("tensor", lambda e: [e.matmul(ps[bank][:, :], w4_t[:, slot, base + kc * 128:base + (kc + 1) * 128],
                                                y_t[:, kc, :], start=(kc == 0), stop=(kc == KC - 1))
                                       for kc in range(KC)][-1],
                  r=[w4B[slot]] + yB, w=pq(bank))
            postnorm_residual(produce, l, 3, False)

        def mixer0(ti):
            prenorm(0, 2)
            A("gpsimd", lambda e: [e.dma_start(out=rope_t[:, 0, :, :], in_=cos_d[:, ti * 4:ti * 4 + 4, :]),
                                   e.dma_start(out=rope_t[:, 1, :, :], in_=sin_d[:, ti * 4:ti * 4 + 4, :])],
              w=[ropeB], key="rope", nd=2)
            UT = [0, 1, 2, 3]
            GT = [4, 5, 6, 7]
            VLN = [8, 9, 10, 11]
            KZ = [16, 17, 18, 19]
            VR = [20, 21, 22, 23]
            QT = [24, 25, 26, 27]
            QX = [28, 29, 30, 31]
            KT = [32, 33, 34, 35]
            SC = 36
            QR = [12, 13, 37, 38]
            KR = [14, 15, 39, 40]
            VTOK = [8, 9, 10, 11]
            QTOK = [12, 13, 14, 15]
            VT = [(fs_t[:, VTOK[b], :], fsB[VTOK[b]]) for b in range(4)]
            QTK = [(fs_t[:, QTOK[b], :], fsB[QTOK[b]]) for b in range(4)]
            KTK = [(f_t[:, b, :], fB[b]) for b in range(4)]
            PT = [(f_t[:, 4 + i, :], fB[4 + i]) for i in range(4)]
            pt_i = [0]
            bank_i = [0]

            def proj_ug():
                for (name, dst, fn) in ((("in_u", 0), UT, AF.Gelu_apprx_tanh), (("in_g", 0), GT, AF.Silu)):
                    for c in range(4):
                        if c % 2 == 0:
                            slot = load_w4(name, c * 1024, 2048)
                        base = (c % 2) * 1024
                        bank = bank_i[0] % 2
                        bank_i[0] += 1
                        A("tensor", (lambda slot, base, bank: (lambda e: [e.matmul(
                            ps[bank][:, :], w4_t[:, slot, base + kc * 128:base + (kc + 1) * 128], h_t[:, kc, :],
                            start=(kc == 0), stop=(kc == KC - 1)) for kc in range(KC)][-1]))(slot, base, bank),
                          r=[w4B[slot]] + hB, w=pq(bank))
                        A("scalar", (lambda bank, d, fn: (lambda e: e.activation(out=bs_t[:, d, :], in_=ps[bank][:, :],
                                                                                func=fn)))(bank, dst[c], fn),
                          r=pq(bank), w=[bsB[dst[c]]])

            hb_i = [0]

            def tokproj(g8, evac):
                slot = load_w4(("in_tok", 0), g8 * 2048, 2048)
                for b in range(4):
                    hb = hb_i[0] % 4
                    hb_i[0] += 1
                    bank, c0 = 2 + hb // 2, (hb % 2) * 256
                    A("tensor", (lambda slot, b, bank, c0: (lambda e: [e.matmul(
                        ps[bank][:, c0:c0 + 256], h_t[:, kc, b * 128:(b + 1) * 128], w4_t[:, slot, kc * 256:(kc + 1) * 256],
                        start=(kc == 0), stop=(kc == KC - 1)) for kc in range(KC)][-1]))(slot, b, bank, c0),
                      r=[w4B[slot]] + hB, w=pq(bank, c0, c0 + 256))
                    evac(b, ps[bank][:, c0:c0 + 256], pq(bank, c0, c0 + 256))

            def proj_tok_f32(g0, dst, fn):
                for g8 in (g0, g0 + 1):
                    c0 = (g8 % 2) * 256
                    tokproj(g8, lambda b, src, sb_, c0=c0: A("scalar", lambda e: e.activation(
                        out=dst[b][0][:, c0:c0 + 256], in_=src, func=fn), r=sb_, w=[dst[b][1]]))

            def ln_v():
                for b in range(4):
                    X = fs_t[:, VTOK[b], :]
                    s1, nm, s2, rs = sm_t[:, 4 * b:4 * b + 1], sm_t[:, 4 * b + 1:4 * b + 2], sm_t[:, 4 * b + 2:4 * b + 3], sm_t[:, 4 * b + 3:4 * b + 4]
                    mB = smB[b]
                    t0 = fs_next()
                    t1 = fs_next()
                    A("vector", (lambda X, s1: (lambda e: e.tensor_reduce(out=s1, in_=X, axis=AX.X, op=ALU.add)))(X, s1),
                      r=[fsB[VTOK[b]]], w=[mB])
                    A("vector", (lambda s1, nm: (lambda e: e.tensor_scalar(out=nm, in0=s1, scalar1=-1.0 / 512, scalar2=0.0,
                                                                          op0=ALU.mult, op1=ALU.add)))(s1, nm), r=[mB], w=[mB])
                    A("scalar", (lambda X, nm, t0: (lambda e: e.activation(out=fs_t[:, t0, :], in_=X, func=AF.Identity,
                                                                           bias=nm, scale=1.0)))(X, nm, t0),
                      r=[fsB[VTOK[b]], mB], w=[fsB[t0]])
                    A("scalar", (lambda t0, t1: (lambda e: e.activation(out=fs_t[:, t1, :], in_=fs_t[:, t0, :],
                                                                        func=AF.Square)))(t0, t1), r=[fsB[t0]], w=[fsB[t1]])
                    A("vector", (lambda t1, s2: (lambda e: e.tensor_reduce(out=s2, in_=fs_t[:, t1, :], axis=AX.X,
                                                                          op=ALU.add)))(t1, s2), r=[fsB[t1]], w=[mB])
                    rstd_op(rs, s2, 1.0 / 512, [mB], [mB])
                    A("vector", (lambda t0, t1, rs: (lambda e: e.scalar_tensor_tensor(
                        out=fs_t[:, t1, :], in0=fs_t[:, t0, :], scalar=rs, in1=cc("LNG"), op0=ALU.mult, op1=ALU.mult)))(t0, t1, rs),
                      r=[fsB[t0], mB, cfB], w=[fsB[t1]])
                    A("vector", (lambda t1, b: (lambda e: e.tensor_tensor(out=bs_t[:, VLN[b], :], in0=fs_t[:, t1, :],
                                                                         in1=cc("LNB"), op=ALU.add)))(t1, b),
                      r=[fsB[t1], cfB], w=[bsB[VLN[b]]])

            def spatial():
                for g in range(4):
                    bank = 4 + g % 2

                    def spat(e, g=g, bank=bank):
                        last = None
                        for b in range(4):
                            o_ = ps[bank][:, b * 128:(b + 1) * 128]
                            e.matmul(o_, bs_t[:, VLN[b], g * 128:(g + 1) * 128], wst_t[:, g, :], start=True, stop=False)
                            e.matmul(o_, ONESb[0:1, :], bsr_t[0:1, 0, g * 128:(g + 1) * 128], start=False, stop=False)
                            last = e.matmul(o_, ONESb[0:1, :], bsr_t[0:1, 1, g * 128:(g + 1) * 128], start=False, stop=True)
                        return last
                    A("tensor", spat, r=[bsB[VLN[b]] for b in range(4)] + [setupB], w=pq(bank))
                    A("vector", (lambda g, bank: (lambda e: e.tensor_tensor(out=y_t[:, g, :], in0=ps[bank][:, :],
                                                                           in1=bs_t[:, UT[g], :], op=ALU.mult)))(g, bank),
                      r=pq(bank) + [bsB[UT[g]]], w=[yB[g]])

            def rotary(eng, src, b, dst_slab):
                src_ap, src_buf = src
                Xv = src_ap.rearrange("p (h t j) -> p h t j", h=4, t=2)
                Ov = bs_t[:, dst_slab, :].rearrange("p (h t j) -> p h t j", h=4, t=2)
                x1, x2 = Xv[:, :, 0, :], Xv[:, :, 1, :]
                cosb = rope_t[:, 0, b, :].unsqueeze(1).broadcast_to([128, 4, 64])
                sinb = rope_t[:, 1, b, :].unsqueeze(1).broadcast_to([128, 4, 64])
                if eng == "gpsimd":
                    (ta_ap, taB), (tb_ap, tbB) = PT[pt_i[0] % 4], PT[(pt_i[0] + 1) % 4]
                    pt_i[0] += 2
                else:
                    ta = fs_next()
                    tb = fs_next()
                    (ta_ap, taB), (tb_ap, tbB) = (fs_t[:, ta, :], fsB[ta]), (fs_t[:, tb, :], fsB[tb])
                Ta = ta_ap.rearrange("p (u h j) -> p u h j", u=2, h=4)
                Tb = tb_ap.rearrange("p (u h j) -> p u h j", u=2, h=4)
                rd = [src_buf, ropeB]
                A(eng, lambda e: e.tensor_tensor(out=Ta[:, 0], in0=x1, in1=cosb, op=ALU.mult), r=rd, w=[taB])
                A(eng, lambda e: e.tensor_tensor(out=Ta[:, 1], in0=x2, in1=sinb, op=ALU.mult), r=rd, w=[taB])
                A(eng, lambda e: e.tensor_tensor(out=Ov[:, :, 0, :], in0=Ta[:, 0], in1=Ta[:, 1], op=ALU.subtract),
                  r=[taB], w=[bsB[dst_slab]])
                A(eng, lambda e: e.tensor_tensor(out=Tb[:, 0], in0=x1, in1=sinb, op=ALU.mult), r=rd, w=[tbB])
                A(eng, lambda e: e.tensor_tensor(out=Tb[:, 1], in0=x2, in1=cosb, op=ALU.mult), r=rd, w=[tbB])
                A(eng, lambda e: e.tensor_tensor(out=Ov[:, :, 1, :], in0=Tb[:, 0], in1=Tb[:, 1], op=ALU.add),
                  r=[tbB], w=[bsB[dst_slab]])

            def transposes(src_slab, b, dsts):
                A("tensor", lambda e: [e.transpose(pst[:, h, :], bs_t[:, src_slab, h * 128:(h + 1) * 128], IDb)
                                       for h in range(4)][-1], r=[bsB[src_slab], setupB], w=[pstB])
                for (d, mul) in dsts:
                    outap = bs_t[:, d[0]:d[0] + 4, b * 128:(b + 1) * 128]
                    if mul is None:
                        A("scalar", (lambda outap: (lambda e: e.activation(out=outap, in_=pst[:, 0:4, :], func=AF.Copy)))(outap),
                          r=[pstB], w=[bsB[x] for x in d])
                    else:
                        A("vector", (lambda outap: (lambda e: e.tensor_tensor(
                            out=outap, in0=pst[:, 0:4, :], in1=cc("XIBC").rearrange("p (h n) -> p h n", h=4),
                            op=ALU.mult)))(outap), r=[pstB, cfB], w=[bsB[x] for x in d])

            proj_tok_f32(4, KTK, AF.Copy)
            proj_tok_f32(2, QTK, AF.Copy)
            proj_tok_f32(0, VT, AF.Gelu_apprx_tanh)
            for g8 in (6, 7):
                c0 = (g8 % 2) * 256
                tokproj(g8, lambda b, src, sb_, c0=c0: A("scalar", lambda e: e.activation(
                    out=bs_t[:, VR[b], c0:c0 + 256], in_=src, func=AF.Copy), r=sb_, w=[bsB[VR[b]]]))
            proj_ug()
            for b in range(4):
                rotary("gpsimd", KTK[b], b, KR[b])
                for h in range(4):
                    A("gpsimd", (lambda h, b: (lambda e: e.tensor_scalar(
                        out=bs_t[:, KZ[b], h * 128:(h + 1) * 128], in0=bs_t[:, KR[b], h * 128:(h + 1) * 128],
                        scalar1=cc("ZETA", h, h + 1), scalar2=0.0, op0=ALU.mult, op1=ALU.add)))(h, b),
                      r=[bsB[KR[b]], cfB], w=[bsB[KZ[b]]])
            for b in range(4):
                rotary("vector", QTK[b], b, QR[b])
            ln_v()
            for b in range(4):
                transposes(QR[b], b, [(QT, None), (QX, True)])
            for b in range(4):
                transposes(KR[b], b, [(KT, None)])
            spatial()
            sc_i = [0]
            for b in range(4):
                for h in range(4):
                    q = sc_i[0] % 2
                    sc_i[0] += 1
                    hs = slice(h * 128, (h + 1) * 128)
                    bsl = slice(b * 128, (b + 1) * 128)
                    A("tensor", (lambda h, q, bsl: (lambda e: e.matmul(ps[q][:, 0:128], bs_t[:, KT[h], bsl],
                                                                      bs_t[:, QT[h], bsl], start=True, stop=True)))(h, q, bsl),
                      r=[bsB[KT[h]], bsB[QT[h]]], w=[psB[q]])
                    scap = bs_t[:, SC, q * 128:(q + 1) * 128]
                    A("vector", (lambda h, q, scap: (lambda e: e.tensor_tensor(
                        out=scap, in0=ps[q][:, 0:128], in1=cc("DECT", h * 128, h * 128 + 128),
                        op=ALU.mult)))(h, q, scap), r=[psB[q], cfB], w=[scB[q]])
                    ob = 2 + h

                    def core(e, h=h, b=b, hs=hs, bsl=bsl, scap=scap, ob=ob):
                        e.matmul(ps[ob][:, bsl], bs_t[:, VR[b], hs], scap, start=True, stop=False)
                        return e.matmul(ps[ob][:, bsl], st_b[:, h, :], bs_t[:, QX[h], bsl], start=False, stop=True)
                    A("tensor", core, r=[bsB[VR[b]], scB[q], stbB[h], bsB[QX[h]]], w=[psB[ob]])
                    A("tensor", (lambda h, hs, b: (lambda e: e.matmul(ps[6][:, 0:128], bs_t[:, KZ[b], hs],
                                                                     bs_t[:, VR[b], hs], start=True, stop=True)))(h, hs, b),
                      r=[bsB[KZ[b]], bsB[VR[b]]], w=[psB[6]])
                    A("vector", (lambda h: (lambda e: e.scalar_tensor_tensor(
                        out=st_f[:, h, :], in0=st_f[:, h, :], scalar=cc("GAMC", h, h + 1), in1=ps[6][:, 0:128],
                        op0=ALU.mult, op1=ALU.add)))(h), r=[stfB[h], psB[6], cfB], w=[stfB[h]])
                    A("gpsimd", (lambda h: (lambda e: e.tensor_copy(out=st_b[:, h, :], in_=st_f[:, h, :])))(h),
                      r=[stfB[h]], w=[stbB[h]])
            for h in range(4):
                ob = 2 + h
                s = SQ[sqi[0] % 2]
                sqi[0] += 1
                R = 4
                t = fs_next()
                A("scalar", (lambda ob, s: (lambda e: e.activation(out=bs_t[:, s, :], in_=ps[ob][:, :], func=AF.Square)))(ob, s),
                  r=pq(ob), w=[bsB[s]])
                A("tensor", (lambda s: (lambda e: e.matmul(ps[6][:, :], ONESb, bs_t[:, s, :], start=True, stop=True)))(s),
                  r=[bsB[s], setupB], w=pq(6))
                op1 = rstd_op(fs_t[:, R, :], ps[6][:, :], 1.0 / 128, pq(6), [fsB[R]], div_ok=True)
                A("vector", (lambda ob, h, t, op1: (lambda e: e.scalar_tensor_tensor(
                    out=fs_t[:, t, :], in0=ps[ob][:, :], scalar=cc("RETG", h, h + 1), in1=fs_t[:, R, :],
                    op0=ALU.mult, op1=op1)))(ob, h, t, op1), r=pq(ob) + [fsB[R], cfB], w=[fsB[t]])
                A("vector", (lambda h, t: (lambda e: e.tensor_tensor(out=y_t[:, 4 + h, :], in0=fs_t[:, t, :],
                                                                    in1=bs_t[:, GT[h], :], op=ALU.mult)))(h, t),
                  r=[fsB[t], bsB[GT[h]]], w=[yB[4 + h]])
            outproj(("out", 0), 0)

        scB = [Buf("sc%d" % i) for i in range(4)]

        def mixer1(ti):
            prenorm(1, 2)
            QTs = list(range(0, 8))
            KTs = list(range(8, 16))
            VCs = list(range(16, 24))
            KH = [24, 25, 26]
            VH = [27, 28, 29]
            bank_i = [0]
            for (name, dst, scl) in ((("q", 1), QTs, 0.125), (("k", 1), KTs, 1.0)):
                for c in range(8):
                    if c % 2 == 0:
                        slot = load_w4(name, c * 1024, 2048)
                    base = (c % 2) * 1024
                    bank = bank_i[0] % 4
                    bank_i[0] += 1
                    A("tensor", (lambda slot, base, bank: (lambda e: [e.matmul(
                        ps[bank][:, :], w4_t[:, slot, base + kc * 128:base + (kc + 1) * 128], h_t[:, kc, :],
                        start=(kc == 0), stop=(kc == KC - 1)) for kc in range(KC)][-1]))(slot, base, bank),
                      r=[w4B[slot]] + hB, w=pq(bank))
                    A("scalar", (lambda bank, d, scl: (lambda e: e.activation(out=bs_t[:, d, :], in_=ps[bank][:, :],
                                                                             func=AF.Copy, scale=scl)))(bank, dst[c], scl),
                      r=pq(bank), w=[bsB[dst[c]]])
            hb_i = [0]
            for g4 in range(4):
                slot = load_w4(("v", 1), g4 * 2048, 2048)
                for b in range(4):
                    hb = hb_i[0] % 8
                    hb_i[0] += 1
                    bank, c0 = hb // 2, (hb % 2) * 256
                    A("tensor", (lambda slot, b, bank, c0: (lambda e: [e.matmul(
                        ps[bank][:, c0:c0 + 256], h_t[:, kc, b * 128:(b + 1) * 128], w4_t[:, slot, kc * 256:(kc + 1) * 256],
                        start=(kc == 0), stop=(kc == KC - 1)) for kc in range(KC)][-1]))(slot, b, bank, c0),
                      r=[w4B[slot]] + hB, w=pq(bank, c0, c0 + 256))
                    d = VCs[2 * b + g4 // 2]
                    dc0 = (g4 % 2) * 256
                    A("scalar", (lambda bank, c0, d, dc0: (lambda e: e.activation(
                        out=bs_t[:, d, dc0:dc0 + 256], in_=ps[bank][:, c0:c0 + 256], func=AF.Copy)))(bank, c0, d, dc0),
                      r=pq(bank, c0, c0 + 256), w=[bsB[d]])
            if ti < NT - 1:
                A("gpsimd", lambda e: [e.dma_start(out=kth_d[c, :, ti * TT:(ti + 1) * TT], in_=bs_t[:, KTs[c], :])
                                       for c in range(8)],
                  r=[bsB[s] for s in KTs], w=[kvB[ti]], key="kst", nd=8)
                A("gpsimd", lambda e: [e.dma_start(
                    out=vh_d[hf * 4:(hf + 1) * 4, :, ti * 4 + b, :].rearrange("c p f -> p c f"),
                    in_=bs_t[:, VCs[2 * b + hf], :].rearrange("p (c f) -> p c f", c=4))
                    for b in range(4) for hf in range(2)],
                  r=[bsB[s] for s in VCs], w=[kvB2[ti]], key="vst", nd=8)
            kh_i = [0]
            SP0 = 30
            A0 = 36
            SS = [42, 43, 44, 45]
            E0 = 8
            PO = 6

            def attn_pair(c):
                A("tensor", lambda e: e.matmul(ps[PO][:, :], ZEROb, bs_t[:, QTs[c], :], start=True, stop=False,
                                               skip_group_check=True), r=[setupB, bsB[QTs[c]]], w=[psB[PO]])
                for k in (0, 2):
                    A("gpsimd", (lambda k: (lambda e: e.memset(bs_t[:, SS[k]:SS[k] + 2, :], 0.0)))(k),
                      w=[bsB[SS[k]], bsB[SS[k] + 1]])
                units = [(kch, blk) for kch in range(ti, -1, -1) for blk in (3, 2, 1, 0)]
                nU = len(units)
                srcs = {}
                scur = [0]

                def hist_load(kch):
                    sl = kh_i[0] % 3
                    kh_i[0] += 1
                    A("sync", lambda e: [e.dma_start(out=bs_t[:, KH[sl], :], in_=kth_d[c, :, kch * TT:(kch + 1) * TT])],
                      r=[kvB[kch]], w=[bsB[KH[sl]]], key=("kh", sl))
                    A("sync", lambda e: [e.dma_start(out=bs_t[:, VH[sl], :].rearrange("p (b f) -> p b f", b=4),
                                                     in_=vh_d[c, :, kch * 4:(kch + 1) * 4, :])],
                      r=[kvB2[kch]], w=[bsB[VH[sl]]], key=("vh", sl))
                    srcs[kch] = sl

                for kch in range(ti - 1, max(ti - 3, -1), -1):
                    hist_load(kch)

                def geom(ui):
                    kch, blk = units[ui]
                    cur = kch == ti
                    q0 = 128 * blk if cur else 0
                    return kch, blk, cur, q0

                def stage1(ui):
                    kch, blk, cur, q0 = geom(ui)
                    if (not cur) and blk == 3 and kch - 2 >= 0:
                        hist_load(kch - 2)
                    zk = 2 * (ui % 3)
                    ks = KTs[c] if cur else KH[srcs[kch]]

                    def zmm(e):
                        for hh in (0, 1):
                            hp = slice(hh * 64, hh * 64 + 64)
                            r_ = e.matmul(ps[zk + hh][:, q0:512], bs_t[hp, ks, blk * 128:(blk + 1) * 128],
                                          bs_t[hp, QTs[c], q0:512], start=True, stop=False, skip_group_check=True)
                        if cur:
                            for hh in (0, 1):
                                r_ = e.matmul(ps[zk + hh][:, q0:q0 + 128], IDb, MBb, start=False, stop=False,
                                              skip_group_check=True)
                        return r_
                    A("tensor", zmm, r=[bsB[ks], bsB[QTs[c]], setupB], w=[psB[zk], psB[zk + 1]])
                    ee = E0 + 2 * (ui % 3)
                    A("scalar", lambda e: e.activation(out=fs_t[:, ee:ee + 2, q0:512], in_=psbig[:, zk:zk + 2, q0:512],
                                                       func=AF.Exp), r=[psB[zk], psB[zk + 1]], w=[fsB[ee], fsB[ee + 1]])

                def stage1b(ui):
                    kch, blk, cur, q0 = geom(ui)
                    ee = E0 + 2 * (ui % 3)
                    sp = SP0 + 2 * (ui % 3)
                    A("scalar", lambda e: e.activation(out=bs_t[:, sp:sp + 2, q0:512], in_=fs_t[:, ee:ee + 2, q0:512],
                                                       func=AF.Ln, bias=1.0, scale=1.0),
                      r=[fsB[ee], fsB[ee + 1]], w=[bsB[sp], bsB[sp + 1]])

                def stage2(ui):
                    kch, blk, cur, q0 = geom(ui)
                    zk = 2 * (ui % 3)
                    sp = SP0 + 2 * (ui % 3)
                    first = ui == 0
                    sc_ = SS[2 * scur[0]]
                    if cur:
                        sn_ = sc_
                    else:
                        scur[0] ^= 1
                        sn_ = SS[2 * scur[0]]

                    def cmm(e):
                        for hh in (0, 1):
                            r_ = e.matmul(ps[zk + hh][:, q0:512], NEGUb, bs_t[:, sp + hh, q0:512], start=False, stop=first,
                                          skip_group_check=True)
                            if not first:
                                r_ = e.matmul(ps[zk + hh][:, q0:512], NEGONESb, bs_t[:, sc_ + hh, q0:512], start=False,
                                              stop=True, skip_group_check=True)
                        return r_
                    A("tensor", cmm, r=[bsB[sp], bsB[sp + 1], bsB[sc_], bsB[sc_ + 1], setupB], w=[psB[zk], psB[zk + 1]])
                    if ui < nU - 1:
                        A("vector", lambda e: e.tensor_tensor(out=bs_t[:, sn_:sn_ + 2, q0:512], in0=bs_t[:, sc_:sc_ + 2, q0:512],
                                                              in1=bs_t[:, sp:sp + 2, q0:512], op=ALU.add),
                          r=[bsB[sc_], bsB[sc_ + 1], bsB[sp], bsB[sp + 1]], w=[bsB[sn_], bsB[sn_ + 1]])

                def stage2b(ui):
                    kch, blk, cur, q0 = geom(ui)
                    zk = 2 * (ui % 3)
                    a_ = A0 + 2 * (ui % 3)
                    A("scalar", lambda e: e.activation(out=bs_t[:, a_:a_ + 2, q0:512], in_=psbig[:, zk:zk + 2, q0:512],
                                                       func=AF.Exp), r=[psB[zk], psB[zk + 1]], w=[bsB[a_], bsB[a_ + 1]])

                def stage3(ui):
                    kch, blk, cur, q0 = geom(ui)
                    a_ = A0 + 2 * (ui % 3)
                    if cur:
                        vs = VCs[2 * blk + c // 4]
                        vc0 = (c % 4) * 128
                    else:
                        vs = VH[srcs[kch]]
                        vc0 = blk * 128
                    last = ui == nU - 1

                    def avmm(e):
                        for hh in (0, 1):
                            hp = slice(hh * 64, hh * 64 + 64)
                            r_ = e.matmul(ps[PO][hp, q0:512], bs_t[:, vs, vc0 + hh * 64:vc0 + hh * 64 + 64],
                                          bs_t[:, a_ + hh, q0:512], start=False, stop=last, tile_position=(0, hh * 64),
                                          skip_group_check=True)
                        return r_
                    A("tensor", avmm, r=[bsB[vs], bsB[a_], bsB[a_ + 1]], w=[psB[PO]])

                for step in range(nU + 3):
                    if step < nU:
                        stage1(step)
                    if 0 <= step - 1 < nU:
                        stage2(step - 1)
                    if 0 <= step - 2 < nU:
                        stage2b(step - 2)
                    if step < nU:
                        stage1b(step)
                    if 0 <= step - 3 < nU:
                        stage3(step - 3)
                A("vector", lambda e: e.tensor_copy(out=y_t[:, c, :], in_=ps[PO][:, :]), r=[psB[PO]], w=[yB[c]])

            for c in range(8):
                attn_pair(c)
            outproj(("out", 1), 1)

        kvB = [Buf("kv%d" % i) for i in range(NT)]
        kvB2 = [Buf("kvv%d" % i) for i in range(NT)]

        xsrc = xT_d.rearrange("(kc p) s -> p kc s", p=128)
        odst = outT_d.rearrange("(kc p) s -> p kc s", p=128)
        last_store = None
        stores = []

        def load_x(ti):
            k = ti % 2
            A("gpsimd", lambda e: [e.dma_start(out=x_ts[k][:], in_=xsrc[:, :, ti * TT:(ti + 1) * TT])], w=xBs[k], key=("xld", k))

        load_x(0)
        for ti in range(NT):
            xcur[0] = ti % 2
            if ti + 1 < NT:
                load_x(ti + 1)
            for si, stg in enumerate(stages):
                if stg == "f00":
                    ffn(0, 0)
                elif stg == "m0":
                    mixer0(ti)
                elif stg == "f01":
                    ffn(0, 1)
                elif stg == "f10":
                    ffn(1, 0)
                elif stg == "m1":
                    mixer1(ti)
                elif stg == "f11":
                    ffn(1, 1)
                if ti == 0 and si + 2 < len(stages):
                    emit_casts(stages[si + 2])
            last_store = A("gpsimd", (lambda ti: (lambda e: [e.dma_start(out=odst[:, :, ti * TT:(ti + 1) * TT],
                                                                        in_=x_ts[ti % 2][:])]))(ti), r=xBs[ti % 2],
                           key=("xst", ti % 2))
            stores.append(last_store)
        sch.finals.extend(stores[-2:])
        block = es.enter_context(nc.Block())
        sch.emit(nc, block, es)
    return nc, sch


_CACHE = {}


def kernel(**inputs):
    x = np.asarray(inputs["x"], dtype=np.float32)
    B, S, _ = x.shape
    inp = {k: np.asarray(v, dtype=np.float32) for k, v in inputs.items()}
    wall = pack_weights(inp)
    consts = pack_consts(inp)
    cos, sin = rope_tables(S)
    nc, _ = build_program(S)
    in_maps = []
    for b in range(B):
        in_maps.append({"xT": np.ascontiguousarray(x[b].T), "wall": wall, "consts": consts, "cos": cos, "sin": sin})
    res = run_bass_kernel_spmd(nc, in_maps, core_ids=list(range(B)))
    out = np.empty((B, S, D), np.float32)
    for b in range(B):
        out[b] = res.results[b]["outT"].T
    return out
```
